# Optimizing a Trainium2 kernel written in Bass

```python
import jax, jax.numpy as jnp
from jax import lax
import numpy as np

D_MODEL = 2048
BATCH = 4
SEQ = 2048
DEPTH = 1
DEC_BATCH = 32
DEC_SEQ = 16
PAST_LEN = 1024

CHUNK = 64
MIX_WIDTH = D_MODEL
CONV_WIDTH = MIX_WIDTH // 2
CONV_KERNEL = 31
CONV_STATE = CONV_KERNEL - 1
HG_WIDTH = MIX_WIDTH - CONV_WIDTH
HG_HEAD_DIM = 128
HG_HEADS = HG_WIDTH // HG_HEAD_DIM
HG_BLOCK = CHUNK // 4
D_FF = ((8 * D_MODEL + 767) // 768) * 256
IN_COLS = 2 * CONV_WIDTH + 4 * HG_WIDTH
SPLITS = [CONV_WIDTH, 2 * CONV_WIDTH, 2 * CONV_WIDTH + HG_WIDTH,
          2 * CONV_WIDTH + 2 * HG_WIDTH, 2 * CONV_WIDTH + 3 * HG_WIDTH]
ALPHA = (2.0 * DEPTH) ** 0.25
BETA = (8.0 * DEPTH) ** -0.25
LN_EPS = 1e-5

kernel_name = "hymba_conformer_hgrn2_stream_step"


def layer_norm(x, g, b):
    xf = x.astype(jnp.float32)
    mu = jnp.mean(xf, -1, keepdims=True)
    var = jnp.mean(jnp.square(xf - mu), -1, keepdims=True)
    return ((xf - mu) * lax.rsqrt(var + LN_EPS) * g.astype(jnp.float32)
            + b.astype(jnp.float32)).astype(x.dtype)


def rms_norm(x, g):
    xf = x.astype(jnp.float32)
    return xf * lax.rsqrt(jnp.mean(jnp.square(xf), -1, keepdims=True) + LN_EPS) * g.astype(jnp.float32)


def conv_mixer(a, gate, hist, w_dw, b_dw, n_g, n_b):
    u = a * jax.nn.sigmoid(gate)
    u_ext = jnp.concatenate([hist.astype(u.dtype), u], axis=1)
    h = lax.conv_general_dilated(
        u_ext, w_dw[:, None, :].astype(u.dtype), window_strides=(1,), padding='VALID',
        dimension_numbers=('NWC', 'WIO', 'NWC'), feature_group_count=u.shape[-1])
    h = h + b_dw
    h = jax.nn.silu(layer_norm(h, n_g, n_b))
    return h, u_ext[:, -CONV_STATE:]


def hgrn2_block(S0, q, k, v, logf):
    L = q.shape[2]
    lc = jnp.cumsum(logf, axis=2)
    causal = jnp.tril(jnp.ones((L, L), dtype=bool))
    diff = lc[:, :, :, None, :] - lc[:, :, None, :, :]
    decay = jnp.exp(jnp.where(causal[None, None, :, :, None], diff, -jnp.inf))
    scores = jnp.einsum('bhtk,bhsk,bhtsk->bhts', q, k, decay)
    o = (jnp.einsum('bhts,bhsv->bhtv', scores, v)
         + jnp.einsum('bhtk,bhkv->bhtv', q * jnp.exp(lc), S0))
    lc_last = lc[:, :, -1:, :]
    k_to_end = k * jnp.exp(lc_last - lc)
    S1 = jnp.exp(lc_last[:, :, 0, :])[..., None] * S0 + jnp.einsum('bhsk,bhsv->bhkv', k_to_end, v)
    return S1, o


def hgrn2_scan(S0, q, k, v, logf):
    B, H, T, _ = q.shape
    n = T // HG_BLOCK

    def to_blocks(z):
        return jnp.moveaxis(z.reshape(B, H, n, HG_BLOCK, z.shape[-1]), 2, 0)

    def step(S, blk):
        return hgrn2_block(S, blk[0], blk[1], blk[2], blk[3])

    S_fin, o = lax.scan(step, S0, (to_blocks(q), to_blocks(k), to_blocks(v), to_blocks(logf)))
    o = jnp.moveaxis(o, 0, 2).reshape(B, H, T, HG_HEAD_DIM)
    return S_fin, o


def hgrn2_mixer(q_raw, f_raw, i_raw, g_raw, S0, lb, norm_g, seq_fn):
    B, T, _ = q_raw.shape

    def heads(z):
        return z.astype(jnp.float32).reshape(B, T, HG_HEADS, HG_HEAD_DIM).transpose(0, 2, 1, 3)

    f = lb + (1.0 - lb) * jax.nn.sigmoid(f_raw.astype(jnp.float32))
    f = heads(f)
    q = jax.nn.silu(heads(q_raw))
    S1, o = seq_fn(S0.astype(jnp.float32), q, 1.0 - f, heads(i_raw), jnp.log(f))
    o = rms_norm(o, norm_g)
    o = o.transpose(0, 2, 1, 3).reshape(B, T, HG_WIDTH) * jax.nn.silu(g_raw.astype(jnp.float32))
    return o.astype(q_raw.dtype), S1


def encoder_layer(x, conv_hist, hg_state, seq_fn, w_in, b_in, w_dw, b_dw, cn_g, cn_b, lb,
                  hg_norm_g, w_out, ln1_g, ln1_b, w_gate, w_up, w_down, ln2_g, ln2_b):
    proj = x @ w_in + b_in
    a, gate, q_raw, f_raw, i_raw, g_raw = jnp.split(proj, SPLITS, axis=-1)
    conv_out, conv_new = conv_mixer(a, gate, conv_hist, w_dw, b_dw, cn_g, cn_b)
    hg_out, hg_new = hgrn2_mixer(q_raw, f_raw, i_raw, g_raw, hg_state, lb, hg_norm_g, seq_fn)
    mixed = jnp.concatenate([conv_out, hg_out], axis=-1) @ w_out
    h = layer_norm(ALPHA * x + mixed, ln1_g, ln1_b)
    ffn = (jax.nn.silu(h @ w_gate) * (h @ w_up)) @ w_down
    y = layer_norm(ALPHA * h + ffn, ln2_g, ln2_b)
    return y, conv_new, hg_new


def setup_inputs(seed: int = 0) -> dict:
    key = jax.random.key(seed)
    ks = jax.random.split(key, 24)
    nrm = lambda k, s, sc: jax.random.normal(k, s, jnp.float32) * sc
    return {
        "x_prompt": nrm(ks[0], (BATCH, SEQ, D_MODEL), 1.0),
        "x_sample": nrm(ks[1], (DEC_BATCH, DEC_SEQ, D_MODEL), 1.0),
        "cache_conv": nrm(ks[2], (DEPTH, DEC_BATCH, CONV_STATE, CONV_WIDTH), 0.5),
        "state_hgrn": nrm(ks[3], (DEPTH, DEC_BATCH, HG_HEADS, HG_HEAD_DIM, HG_HEAD_DIM), 0.5),
        "w_in": nrm(ks[4], (DEPTH, D_MODEL, IN_COLS), D_MODEL ** -0.5),
        "b_in": nrm(ks[5], (DEPTH, IN_COLS), 0.02),
        "w_dw": nrm(ks[6], (DEPTH, CONV_KERNEL, CONV_WIDTH), CONV_KERNEL ** -0.5),
        "b_dw": nrm(ks[7], (DEPTH, CONV_WIDTH), 0.02),
        "conv_norm_g": 1.0 + nrm(ks[8], (DEPTH, CONV_WIDTH), 0.02),
        "conv_norm_b": nrm(ks[9], (DEPTH, CONV_WIDTH), 0.02),
        "hg_lower_bounds": nrm(ks[10], (DEPTH + 1, HG_WIDTH), 0.5),
        "hg_norm_g": 1.0 + nrm(ks[11], (DEPTH, HG_HEAD_DIM), 0.02),
        "w_out": nrm(ks[12], (DEPTH, MIX_WIDTH, D_MODEL), BETA * MIX_WIDTH ** -0.5),
        "ln1_g": 1.0 + nrm(ks[13], (DEPTH, D_MODEL), 0.02),
        "ln1_b": nrm(ks[14], (DEPTH, D_MODEL), 0.02),
        "w_gate": nrm(ks[15], (DEPTH, D_MODEL, D_FF), D_MODEL ** -0.5),
        "w_up": nrm(ks[16], (DEPTH, D_MODEL, D_FF), D_MODEL ** -0.5),
        "w_down": nrm(ks[17], (DEPTH, D_FF, D_MODEL), BETA * D_FF ** -0.5),
        "ln2_g": 1.0 + nrm(ks[18], (DEPTH, D_MODEL), 0.02),
        "ln2_b": nrm(ks[19], (DEPTH, D_MODEL), 0.02),
    }


def reference(x_prompt, x_sample, cache_conv, state_hgrn, w_in, b_in, w_dw, b_dw, conv_norm_g,
              conv_norm_b, hg_lower_bounds, hg_norm_g, w_out, ln1_g, ln1_b, w_gate, w_up, w_down,
              ln2_g, ln2_b):
    lb_all = jnp.cumsum(jax.nn.softmax(hg_lower_bounds.astype(jnp.float32), axis=0), axis=0)
    yp, ys = x_prompt, x_sample
    conv_p, hg_p, conv_s, hg_s = [], [], [], []
    for l in range(DEPTH):
        params = (w_in[l], b_in[l], w_dw[l], b_dw[l], conv_norm_g[l], conv_norm_b[l], lb_all[l],
                  hg_norm_g[l], w_out[l], ln1_g[l], ln1_b[l], w_gate[l], w_up[l], w_down[l],
                  ln2_g[l], ln2_b[l])
        hist0 = jnp.zeros((yp.shape[0], CONV_STATE, CONV_WIDTH), yp.dtype)
        S0 = jnp.zeros((yp.shape[0], HG_HEADS, HG_HEAD_DIM, HG_HEAD_DIM), jnp.float32)
        yp, cp, sp = encoder_layer(yp, hist0, S0, hgrn2_scan, *params)
        ys, cs, ss = encoder_layer(ys, cache_conv[l], state_hgrn[l], hgrn2_block, *params)
        conv_p.append(cp)
        hg_p.append(sp.astype(x_prompt.dtype))
        conv_s.append(cs.astype(cache_conv.dtype))
        hg_s.append(ss.astype(state_hgrn.dtype))
    return (yp, ys, jnp.stack(conv_p), jnp.stack(hg_p), jnp.stack(conv_s), jnp.stack(hg_s))
```

```python
import numpy as np
from contextlib import ExitStack
import concourse.bass as bass
import concourse.mybir as mybir
from concourse.bass_utils import run_bass_kernel_spmd

F32 = mybir.dt.float32
BF16 = mybir.dt.bfloat16
AF = mybir.ActivationFunctionType
ALU = mybir.AluOpType

D = 2048
DFF = 5632
NH = 8
TM = 512
NS = 2
T = TM + 16 * NS
TT = [(0, 272), (272, 272)]
ALPHA = 2.0 ** 0.25
EPS = 1e-5
NSLOT = 6
NUNIT = 196
UEW = 640
SMP0 = 30 + TM
SAME_ENG_SYNC = True
DEBUG = False
USE_SCAN = True
HG_PIPE = True

R_BIN, R_WDW, R_BDW, R_CNG, R_CNB, R_LB0, R_LB1, R_HNG = 0, 48, 296, 304, 312, 320, 328, 336
R_L1G, R_L1B, R_L2G, R_L2B = 337, 353, 369, 385
NVEC = 512
C_ID, C_CAUS, C_SMASK, C_SEQM, C_RMASK, C_PRM, NCONST = 0, 128, 192, 224, 226, 770, 772


class Op:
    __slots__ = ("eng", "fn", "deps", "mark", "val", "dma", "dmaval", "dmadeps")


class Sched:
    ENG = ("pe", "act", "dve", "pool", "sp")

    def __init__(self):
        self.q = {e: [] for e in self.ENG}
        self.lastw = {}
        self.rd = {}
        self.dma_cnt = {}

    def op(self, eng, fn, r=(), w=(), dma=None):
        o = Op()
        o.eng, o.fn, o.deps, o.mark, o.val, o.dma, o.dmaval = eng, fn, [], False, 0, dma, 0
        deps = {}
        for k in list(r) + list(w):
            lw = self.lastw.get(k)
            if lw is not None:
                deps[id(lw)] = lw
        for k in w:
            for rr in self.rd.get(k, {}).values():
                deps[id(rr)] = rr
        o.deps = [d for d in deps.values() if d.dma is None]
        o.dmadeps = {d.dma: 16 * self.dma_cnt[d.dma] for d in deps.values() if d.dma is not None}
        for k in r:
            d = self.rd.setdefault(k, {})
            d[eng if dma is None else ("dma", id(o))] = o
        for k in w:
            self.lastw[k] = o
            self.rd[k] = {}
        if dma is not None:
            self.dma_cnt[dma] = self.dma_cnt.get(dma, 0) + 1
            o.dmaval = 16 * self.dma_cnt[dma]
        self.q[eng].append(o)
        return o

    def finalize(self):
        for e in self.ENG:
            for o in self.q[e]:
                for d in o.deps:
                    if d.dma is None:
                        d.mark = True
        for e in self.ENG:
            c = 0
            for o in self.q[e]:
                if o.mark and o.dma is None:
                    c += 1
                    o.val = c

    def replay(self, ename, eng, prog, dmasems):
        waited = {}
        for o in self.q[ename]:
            wl = [(("d", st), dmasems[st], v) for st, v in o.dmadeps.items()]
            for d in o.deps:
                if d.eng == ename and (ename == "pe" or not SAME_ENG_SYNC):
                    continue
                wl.append((("p", d.eng), prog[d.eng], d.val))
            for key, sem, val in wl:
                if waited.get(key, 0) >= val:
                    continue
                eng.wait_ge(sem, val)
                waited[key] = val
            if o.fn is None:
                continue
            ins = o.fn(eng)
            if o.dma is not None:
                ins.then_inc(dmasems[o.dma], 16)
            elif o.mark:
                ins.then_inc(prog[ename], 1)


def build_nc(order=None):
    dry = order is None
    import os
    SUB = int(os.environ.get("KSUB", "9"))
    nc = bass.Bass("TRN2", target_bir_lowering=False)
    S = Sched()
    es = ExitStack()

    def din(name, shape):
        return nc.dram_tensor(name, shape, F32, kind="ExternalInput").ap()

    def dout(name, shape):
        return nc.dram_tensor(name, shape, F32, kind="ExternalOutput").ap()

    xm = din("xm", [1024, D]); xh = din("xh", [32, D]); xp = din("xp", [1024, D]); xs = din("xs", [64, D])
    flag_d = din("flag", [128, 1]); cc = din("cc", [4, 30, 1024]); sh = din("sh", [4, NH, 128, 128])
    vecs = din("vecs", [NVEC, 128]); consts_d = din("consts", [128, NCONST])
    wall = din("wall", [NUNIT, 128, 2048])
    ym = dout("ym", [1024, D]); ys = dout("ys", [64, D]); cm = dout("cm", [30, 1024])
    hm = dout("hm", [NH, 128, 128]); cs_o = dout("cs", [4, 30, 1024]); hs_o = dout("hs", [4, NH, 128, 128])

    def sb(name, shape, dt=F32):
        return es.enter_context(nc.sbuf_tensor(name, shape, dt))

    RX = sb("RX", [128, 16 * T]); RB = sb("RB", [128, 16 * T], BF16); RC = sb("RC", [128, 16 * T])
    CAT = sb("CAT", [128, 16, T], BF16); SCR = sb("SCR", [128, 4, T], BF16)
    UE = sb("UE", [128, 2, UEW]); BT = sb("BT", [128, 4, T], BF16)
    VB = sb("VB", [128, 9, 128], BF16); KEB = sb("KEB", [128, 9, 128], BF16); KES = sb("KES", [128, 2, 128], BF16)
    S32 = sb("S32", [128, NH, 128]); SBF = sb("SBF", [128, NH, 128], BF16)
    SS0 = sb("SS0", [128, 2, 2, 128]); SS0B = sb("SS0B", [128, 2, 2, 128], BF16); SS1 = sb("SS1", [128, 2, 2, 128])
    WR = sb("WR", [128, NSLOT, 2048], BF16)
    XIN = sb("XIN", [128, 2, D])
    CST = sb("CST", [128, 2, 384]); SPP = sb("SPP", [128, 2, 128]); SBP = sb("SBP", [128, 2, 128], BF16)
    CON = sb("CON", [128, NCONST]); IDB = sb("IDB", [128, 128], BF16)
    ONC = sb("ONC", [128, 128], BF16); OND = sb("OND", [128, 128], BF16); ONH = sb("ONH", [128, 128], BF16)
    VEC = sb("VEC", [128, NVEC]); DER = sb("DER", [128, 64]); FLG = sb("FLG", [128, 1])
    CACHE = sb("CACHE", [128, 8, 4, 30]); UH = sb("UH", [128, 2, 8, 30]); UHS = sb("UHS", [128, 2, 32])
    LNS = sb("LNS", [128, 3, T]); DBL = sb("DBL", [128, 16]); SCM = sb("SCM", [128, 2, 64], BF16)
    XHT = sb("XHT", [128, 16, 32], BF16)
    UEB = sb("UEB", [128, 2, UEW], BF16); DG = sb("DG", [128, 31, 128], BF16)
    PS = es.enter_context(nc.psum_tensor("PS", [128, 8, 512], F32))

    XT32 = RX[:, :].rearrange("p (a b) -> p a b", a=16)
    XTB = RB[:, :].rearrange("p (a b) -> p a b", a=16)
    CP = RC[:, 0:8 * T].rearrange("p (a b) -> p a b", a=8)
    TS = RC[:, 8 * T:16 * T].rearrange("p (a b) -> p a b", a=8)
    ACTB = RC[:, 0:4 * T].bitcast(BF16).rearrange("p (g f t) -> p g f t", g=2, f=4)
    XPT = RX[:, :].bitcast(BF16)[:, 0:16 * 1024].rearrange("p (a b) -> p a b", a=16)
    PT = RC[:, 0:3 * 1024].rearrange("p (a b) -> p a b", a=3)
    PTB = RC[:, 3 * 1024:5 * 1024].bitcast(BF16).rearrange("p (a b) -> p a b", a=4)
    PVB = RC[:, 5 * 1024:6 * 1024].bitcast(BF16).rearrange("p (a b) -> p a b", a=16)

    def vcol(r):
        return VEC[:, r:r + 1]

    ring = [0]

    def ring_next():
        r = ring[0]
        ring[0] = (r + 1) % 8
        return r

    tpi = [0]

    def tp_next():
        r = tpi[0]
        tpi[0] = (r + 1) % 4
        return r

    evt = [0]

    def ev_eng():
        evt[0] ^= 1
        return "act" if evt[0] else "dve"

    def copy_op(eng, out, in_, r, w):
        if eng == "act":
            return S.op("act", lambda e: e.activation(out=out, in_=in_, func=AF.Identity), r=r, w=w)
        return S.op(eng, lambda e: e.tensor_copy(out=out, in_=in_), r=r, w=w)

    wq = [] if dry else list(order)
    wrec = []
    wstate = {"issued": 0}

    def wissue(upto):
        while wstate["issued"] < min(upto, len(wq)):
            i = wstate["issued"]
            uid = wq[i]
            slot = i % NSLOT
            S.op("pool", lambda e, slot=slot, uid=uid: e.dma_start(out=WR[:, slot, :], in_=wall[uid, :, :]),
                 w=[f"W{slot}"], dma=f"W{slot}")
            wstate["issued"] += 1

    wcons = [0]
    wlook = [NSLOT]

    def wnext(a, uid):
        i = wcons[0]
        wcons[0] += 1
        wrec.append(uid)
        if dry:
            wq.append(uid)
            wissue(i + 1)
        else:
            assert wq[i] == uid, (i, wq[i], uid)
            wissue(i + wlook[0])
        slot = i % NSLOT
        return WR[:, slot, :].rearrange("p (a b) -> p a b", a=a), f"W{slot}"

    U_IN, U_OUT, U_GATE, U_UP, U_DOWN = 0, 48, 64, 108, 152

    def mm_group(out, pairs, r, w):
        def fn(e, out=out, pairs=pairs):
            n = len(pairs)
            ins = None
            for i, (l, rh) in enumerate(pairs):
                ins = e.matmul(out, lhsT=l, rhs=rh, start=(i == 0), stop=(i == n - 1))
            return ins
        return S.op("pe", fn, r=r, w=w)

    def proj(wv, wkey, nk, rhs_fn, rkeys, c0, n, wsub=None):
        rg = ring_next()
        out = PS[:, rg, 0:n]
        pairs = [((wv[:, kc, :] if wsub is None else wv[:, kc, wsub[0]:wsub[1]]), rhs_fn(kc, c0, n)) for kc in range(nk)]
        mm_group(out, pairs, r=[wkey] + rkeys, w=[f"ps{rg}"])
        return out, f"ps{rg}"

    def transposes(items, ident, dt, rkeys):
        rg = ring_next()
        bank = PS[:, rg, :] if dt == F32 else PS[:, rg, :].bitcast(BF16)

        def fn(e):
            ins = None
            for (in_ap, np_, nf, col) in items:
                ins = e.transpose(bank[0:nf, col:col + np_], in_ap, ident[0:np_, 0:np_])
            return ins
        S.op("pe", fn, r=rkeys, w=[f"ps{rg}"])
        return bank, f"ps{rg}"

    def cumsum_steps(src, bufa, bufb, ka, kb, ksrc, blk, nblk):
        v = lambda ap: ap.rearrange("p (j t) -> p j t", t=blk)
        cur, ck = src, ksrc
        dsts = [(bufa, ka), (bufb, kb)]
        i, sft = 0, 1
        while sft < blk:
            dst, dk = dsts[i % 2]
            S.op("dve", lambda e, cur=cur, dst=dst, sft=sft: e.tensor_tensor(out=v(dst)[:, :, 0:sft], in0=v(cur)[:, :, 0:sft], in1=v(cur)[:, :, 0:sft], op=ALU.max), r=[ck], w=[dk])
            S.op("dve", lambda e, cur=cur, dst=dst, sft=sft: e.tensor_tensor(out=v(dst)[:, :, sft:blk], in0=v(cur)[:, :, sft:blk], in1=v(cur)[:, :, 0:blk - sft], op=ALU.add), r=[ck], w=[dk])
            cur, ck = dst, dk
            i += 1
            sft *= 2
        return cur

    S.op("sp", lambda e: e.dma_start(out=CON[:, :], in_=consts_d[:, :]), w=["CON"], dma="cin")
    S.op("sp", lambda e: e.dma_start(out=FLG[:, :], in_=flag_d[:, :]), w=["FLG"], dma="fin")
    S.op("act", lambda e: e.activation(out=IDB[:, :], in_=CON[:, C_ID:C_ID + 128], func=AF.Identity), r=["CON"], w=["IDB"])
    S.op("pool", lambda e: e.memset(ONC[:, :], 1.0 / 1024.0), w=["ONC"])
    S.op("pool", lambda e: e.memset(OND[:, :], 1.0 / 2048.0), w=["OND"])
    S.op("pool", lambda e: e.memset(ONH[:, :], 1.0 / 128.0), w=["ONH"])
    S.op("pool", lambda e: e.memset(DER[:, 56:57], EPS), w=["EPSK"])
    S.op("pool", lambda e: e.memset(UE[:, :, :], 0.0), w=["UE0", "UE1"])
    ID32 = CON[:, C_ID:C_ID + 128]
    for i in range(4 if SUB >= 2 else 0):
        S.op("sp", lambda e, i=i: e.dma_start(out=XIN[:, i // 2, (i % 2) * 128:(i % 2) * 128 + 128], in_=vecs[i * 128:(i + 1) * 128, :]),
             w=[f"XIN{i // 2}"], dma=f"xin{i // 2}")
    bk, k = transposes([(XIN[:, i // 2, (i % 2) * 128:(i % 2) * 128 + 128], 128, 128, i * 128) for i in range(4)], ID32, F32, ["XIN0", "XIN1", "CON"])
    copy_op("dve", VEC[:, :], bk[:, 0:512], r=[k], w=["VEC"])
    if SUB >= 3:
      S.op("dve", lambda e: e.tensor_tensor(out=DER[:, 0:8], in0=VEC[:, R_LB0:R_LB0 + 8], in1=VEC[:, R_LB1:R_LB1 + 8], op=ALU.subtract), r=["VEC"], w=["DER"])
      S.op("act", lambda e: e.activation(out=DER[:, 0:8], in_=DER[:, 0:8], func=AF.Sigmoid), r=["DER"], w=["DER"])
      S.op("act", lambda e: e.activation(out=DER[:, 8:16], in_=DER[:, 0:8], func=AF.Identity, scale=-1.0, bias=1.0), r=["DER"], w=["DER"])
      S.op("act", lambda e: e.activation(out=DER[:, 16:24], in_=DER[:, 0:8], func=AF.Identity, scale=1.0, bias=-1.0), r=["DER"], w=["DER"])
      S.op("act", lambda e: e.activation(out=DER[:, 24:56], in_=VEC[:, R_L1G:R_L1G + 32], func=AF.Identity, scale=ALPHA), r=["VEC"], w=["DER"])
    for s in range(4 if SUB >= 4 else 0):
        S.op("sp", lambda e, s=s: e.dma_start(out=XIN[0:30, s % 2, 0:1024], in_=cc[s, :, :]), w=[f"XIN{s % 2}"], dma=f"xin{s % 2}")
        bk, k = transposes([(XIN[0:32, s % 2, c * 128:(c + 1) * 128], 32, 128, c * 32) for c in range(8)], ID32, F32, [f"XIN{s % 2}", "CON"])
        copy_op(ev_eng(), CACHE[:, :, s, :], bk[:, 0:256].rearrange("p (c j) -> p c j", c=8)[:, :, 0:30], r=[k], w=["CACHE"])

    xbuf = [0]

    def load_xT(src, nrows, dstb, dst32, col0, wkeys):
        b = xbuf[0]
        xbuf[0] ^= 1
        S.op("sp", lambda e: e.dma_start(out=XIN[0:nrows, b, :], in_=src), w=[f"XIN{b}"], dma=f"xin{b}")
        for g in range(4):
            rg = ring_next()

            def fn(e, g=g, rg=rg):
                ins = None
                for q in range(4):
                    kc = 4 * g + q
                    ins = e.transpose(PS[:, rg, q * 128:q * 128 + nrows], XIN[0:nrows, b, kc * 128:(kc + 1) * 128], ID32[0:nrows, 0:nrows])
                return ins
            S.op("pe", fn, r=[f"XIN{b}", "CON"], w=[f"ps{rg}"])
            src_ps = PS[:, rg, :].rearrange("p (q c) -> p q c", q=4)[:, :, 0:nrows]
            ce = ev_eng()
            copy_op(ce, dstb[:, 4 * g:4 * g + 4, col0:col0 + nrows], src_ps, r=[f"ps{rg}"], w=[wkeys[0] + str(kc) for kc in range(4 * g, 4 * g + 4)])
            if dst32 is not None:
                copy_op(ce, dst32[:, 4 * g:4 * g + 4, col0:col0 + nrows], src_ps, r=[f"ps{rg}"], w=[wkeys[1] + str(kc) for kc in range(4 * g, 4 * g + 4)])

    def emit_prefix():
        for ti in range(8):
            load_xT(xp[ti * 128:(ti + 1) * 128, :], 128, XPT, None, ti * 128, ["X"])
        XK = [f"X{k}" for k in range(16)]
        PSUB = int(os.environ.get("KPSUB", "9"))
        for h in range(NH if PSUB >= 2 else 0):
            wf, kf = wnext(16, U_IN + 24 + h)
            for (c0, n) in ((0, 512), (512, 512)):
                o, k = proj(wf, kf, 16, lambda kc, c0, n: XPT[:, kc, c0:c0 + n], XK, c0, n)
                S.op("act", lambda e, o=o, c0=c0, n=n, h=h: e.activation(out=PT[:, 0, c0:c0 + n], in_=o, func=AF.Sigmoid, bias=vcol(R_BIN + 24 + h)), r=[k, "VEC"], w=["T0"])
            wi, ki = wnext(16, U_IN + 32 + h)
            for (c0, n) in ((0, 512), (512, 512)):
                o, k = proj(wi, ki, 16, lambda kc, c0, n: XPT[:, kc, c0:c0 + n], XK, c0, n)
                S.op("act", lambda e, o=o, c0=c0, n=n, h=h: e.activation(out=PTB[:, 0, c0:c0 + n], in_=o, func=AF.Identity, bias=vcol(R_BIN + 32 + h)), r=[k, "VEC"], w=["T3"])
            if PSUB < 3:
                continue
            if int(os.environ.get("KP3", "9")) >= 1:
                S.op("act", lambda e, h=h: e.activation(out=PT[:, 1, :], in_=PT[:, 0, :], func=AF.Ln, scale=DER[:, 8 + h:9 + h], bias=DER[:, h:h + 1]), r=["T0", "DER"], w=["T1"])
            S.op("act", lambda e, h=h: e.activation(out=PT[:, 0, :], in_=PT[:, 0, :], func=AF.Identity, scale=DER[:, 16 + h:17 + h], bias=DER[:, 8 + h:9 + h]), r=["T0", "DER"], w=["T0"])
            if int(os.environ.get("KP3", "9")) < 2:
                continue
            PX = RC[:, 4096:5120]
            if USE_SCAN:
                S.op("dve", lambda e: e.tensor_tensor_scan(out=PT[:, 2, :], data0=CON[:, C_PRM:C_PRM + 1].to_broadcast([128, 1024]), data1=PT[:, 1, :], initial=0.0, op0=ALU.mult, op1=ALU.add), r=["T1", "CON"], w=["T2"])
            else:
                cumsum_steps(PT[:, 1, :], PX, PT[:, 2, :], "T7", "T2", "T1", 1024, 1)
            if int(os.environ.get("KP3", "9")) < 3:
                continue
            S.op("act", lambda e: e.activation(out=PT[:, 1, :], in_=PT[:, 2, :], func=AF.Exp, scale=-1.0, bias=PT[:, 2, 1023:1024]), r=["T2"], w=["T1"])
            S.op("dve", lambda e: e.tensor_tensor(out=PTB[:, 1, :], in0=PT[:, 0, :], in1=PT[:, 1, :], op=ALU.mult), r=["T0", "T1"], w=["T4"])
            if PSUB < 4:
                continue
            bk, k = transposes([(PTB[:, 0, ti * 128:(ti + 1) * 128], 128, 128, ti * 128) for ti in range(8)], IDB, BF16, ["T3", "IDB"])
            copy_op(ev_eng(), PVB[:, 0:8, :], bk[:, 0:1024].rearrange("p (a b) -> p a b", a=8), r=[k], w=["T5"])
            bk, k = transposes([(PTB[:, 1, ti * 128:(ti + 1) * 128], 128, 128, ti * 128) for ti in range(8)], IDB, BF16, ["T4", "IDB"])
            copy_op(ev_eng(), PVB[:, 8:16, :], bk[:, 0:1024].rearrange("p (a b) -> p a b", a=8), r=[k], w=["T6"])
            rg = ring_next()
            su = PS[:, rg, 0:128]
            mm_group(su, [(PVB[:, 8 + ti, :], PVB[:, ti, :]) for ti in range(8)], r=["T5", "T6"], w=[f"ps{rg}"])
            S.op("dve", lambda e, h=h, su=su: e.tensor_scalar(out=S32[:, h, :], in0=su, scalar1=FLG[:, 0:1], scalar2=None, op0=ALU.mult), r=[f"ps{rg}", "FLG"], w=[f"S32_{h}"])
            S.op("act", lambda e, h=h: e.activation(out=SBF[:, h, :], in_=S32[:, h, :], func=AF.Identity), r=[f"S32_{h}"], w=[f"SBF_{h}"])
        allk = [f"X{k}" for k in range(16)] + [f"T{k}" for k in range(8)] + [f"CP{k}" for k in range(8)] + ["GS_0"]
        S.op("act", lambda e: e.activation(out=DBL[:, 15:16], in_=DER[:, 56:57], func=AF.Identity), r=["EPSK"], w=allk + ["DBLx"])

    def emit_pass(p):
        XK = [f"B{k}" for k in range(16)]
        _emit_pass_body(p, XK)

    def _emit_pass_body(p, XK):
        for ti in range(4):
            load_xT(xm[p * TM + ti * 128: p * TM + (ti + 1) * 128, :], 128, XTB, XT32, ti * 128, ["B", "X"])
        load_xT(xs[p * 32:(p + 1) * 32, :], 32, XTB, XT32, TM, ["B", "X"])
        if p == 0:
            load_xT(xh[:, :], 32, XHT, None, 0, ["XH"])
        rhsx = lambda kc, c0, n: XTB[:, kc, c0:c0 + n]

        KPASS = int(os.environ.get("KPASS", "9"))
        KCH = int(os.environ.get("KCH", "9"))
        if KPASS < 1:
            return
        SG = TS[:, 6, :]
        def conv_chunk(c):
            ub = c % 2
            ue = UE[:, ub, :]
            uek = f"UE{ub}"
            wb, kb = wnext(16, U_IN + 8 + c)
            for (c0, n) in TT:
                o, k = proj(wb, kb, 16, rhsx, XK, c0, n)
                S.op("act", lambda e, o=o, c0=c0, n=n, c=c: e.activation(out=SG[:, c0:c0 + n], in_=o, func=AF.Sigmoid, bias=vcol(R_BIN + 8 + c)), r=[k, "VEC"], w=["T6"])
            if p == 0:
                o, k = proj(wb, kb, 16, lambda kc, c0, n: XHT[:, kc, 0:32], [f"XH{q}" for q in range(16)], 0, 32)
                S.op("act", lambda e, o=o, c=c: e.activation(out=UHS[:, 0, :], in_=o, func=AF.Sigmoid, bias=vcol(R_BIN + 8 + c)), r=[k, "VEC"], w=["UHS0"])
            yield
            wa, ka = wnext(16, U_IN + c)
            if p == 1:
                S.op("act", lambda e, ue=ue, c=c: e.activation(out=ue[:, 0:30], in_=UH[:, p, c, :], func=AF.Identity), r=[f"UH{p}_{c}"], w=[uek])
            S.op("act", lambda e, ue=ue, c=c: e.activation(out=ue[:, SMP0:SMP0 + 92].rearrange("p (s j) -> p s j", s=2)[:, :, 0:30], in_=CACHE[:, c, 2 * p:2 * p + 2, :], func=AF.Identity), r=["CACHE"], w=[uek])
            for (c0, n) in TT:
                o, k = proj(wa, ka, 16, rhsx, XK, c0, n)
                nm = min(c0 + n, TM) - c0
                if nm > 0:
                    S.op("dve", lambda e, o=o, c0=c0, nm=nm, ue=ue, c=c: e.scalar_tensor_tensor(out=ue[:, 30 + c0:30 + c0 + nm], in0=o[:, 0:nm], scalar=vcol(R_BIN + c), in1=SG[:, c0:c0 + nm], op0=ALU.add, op1=ALU.mult), r=[k, "T6", "VEC"], w=[uek])
                if c0 + n > TM:
                    so = max(TM - c0, 0)
                    S.op("dve", lambda e, o=o, so=so, ue=ue, c=c: e.scalar_tensor_tensor(
                        out=ue[:, SMP0:SMP0 + 92].rearrange("p (s j) -> p s j", s=2)[:, :, 30:46],
                        in0=o[:, so:so + 32].rearrange("p (s j) -> p s j", s=2), scalar=vcol(R_BIN + c),
                        in1=SG[:, TM:TM + 32].rearrange("p (s j) -> p s j", s=2), op0=ALU.add, op1=ALU.mult), r=[k, "T6", "VEC"], w=[uek])
            if p == 0:
                o, k = proj(wa, ka, 16, lambda kc, c0, n: XHT[:, kc, 0:32], [f"XH{q}" for q in range(16)], 0, 32)
                S.op("dve", lambda e, o=o, c=c: e.scalar_tensor_tensor(out=UHS[:, 1, :], in0=o, scalar=vcol(R_BIN + c), in1=UHS[:, 0, :], op0=ALU.add, op1=ALU.mult), r=[k, "UHS0", "VEC"], w=["UHS1"])
                S.op("act", lambda e, c=c: e.activation(out=UH[:, 0, c, :], in_=UHS[:, 1, 2:32], func=AF.Identity, scale=FLG[:, 0:1]), r=["UHS1", "FLG"], w=[f"UH0_{c}"])
                S.op("act", lambda e, ue=ue, c=c: e.activation(out=ue[:, 0:30], in_=UH[:, 0, c, :], func=AF.Identity), r=[f"UH0_{c}"], w=[uek])
            S.op("act", lambda e, ue=ue, ub=ub: e.activation(out=UEB[:, ub, 0:634], in_=ue[:, 0:634], func=AF.Identity), r=[uek], w=[f"UEB{ub}"])

            def conv_part(c=c, ub=ub):
                wtap = VEC[:, R_WDW:R_WDW + 248].rearrange("p (j c) -> p j c", c=8)[:, :, c]
                S.op("pool", lambda e, wtap=wtap: e.tensor_tensor(out=DG[:, :, :], in0=IDB[:, :].unsqueeze(1).to_broadcast([128, 31, 128]),
                                                                   in1=wtap.unsqueeze(2).to_broadcast([128, 31, 128]), op=ALU.mult), r=["IDB", "VEC"], w=["DG"])
                for (c0, n) in TT:
                    rg = ring_next()
                    nm = min(c0 + n, TM) - c0

                    def fn(e, rg=rg, c0=c0, n=n, nm=nm, ub=ub):
                        ins = None
                        for j in range(31):
                            ins = e.matmul(PS[:, rg, 0:nm], lhsT=DG[:, j, :], rhs=UEB[:, ub, c0 + j:c0 + j + nm], start=(j == 0), stop=(j == 30))
                        if nm < n:
                            for s_ in range(2):
                                for j in range(31):
                                    ins = e.matmul(PS[:, rg, nm + 16 * s_:nm + 16 * s_ + 16], lhsT=DG[:, j, :],
                                                   rhs=UEB[:, ub, SMP0 + 46 * s_ + j:SMP0 + 46 * s_ + j + 16], start=(j == 0), stop=(j == 30))
                        return ins
                    S.op("pe", fn, r=["DG", f"UEB{ub}"], w=[f"ps{rg}"])
                    S.op("act", lambda e, rg=rg, c0=c0, n=n, c=c: e.activation(out=CP[:, c, c0:c0 + n], in_=PS[:, rg, 0:n], func=AF.Identity, bias=vcol(R_BDW + c)), r=[f"ps{rg}", "VEC"], w=[f"CP{c}"])
            yield
            conv_part()
            yield
            S.op("act", lambda e, ue=ue, c=c: e.activation(out=UH[:, 1 - p, c, :], in_=ue[:, TM:TM + 30], func=AF.Identity), r=[uek], w=[f"UH{1 - p}_{c}"])
            outs = [(ue[:, SMP0 + 46 * s + 16: SMP0 + 46 * s + 48], cs_o[2 * p + s, :, c * 128:(c + 1) * 128], [uek]) for s in range(2)]
            if p == 1:
                outs.append((UH[:, :, :, :].rearrange("p a b c -> p (a b c)")[:, c * 30:c * 30 + 32], cm[:, c * 128:(c + 1) * 128], [f"UH0_{c}"]))
            rks = ["CON"]
            for (_, _, rk) in outs:
                rks += rk
            bk, k = transposes([(src, 128, 32, i * 128) for i, (src, _, _) in enumerate(outs)], ID32, F32, rks)
            cb = tp_ctr[0] % 2
            tp_ctr[0] += 1
            no = len(outs)
            copy_op(ev_eng(), CST[0:32, cb, 0:no * 128], bk[0:32, 0:no * 128], r=[k], w=[f"CST{cb}"])
            for i, (_, dst, _) in enumerate(outs):
                S.op("sp", lambda e, cb=cb, dst=dst, i=i: e.dma_start(out=dst, in_=CST[0:30, cb, i * 128:(i + 1) * 128]), r=[f"CST{cb}"], dma=f"cst{cb}")
            out_dmas.append(f"cst{cb}")
        def conv_ln():
            layer_norm(lambda c: CP[:, c, :], [f"CP{c}" for c in range(8)], 8, ONC, "ONC",
                       lambda c, t, tk: S.op("act", lambda e: e.activation(out=CAT[:, c, :], in_=t, func=AF.Silu, scale=vcol(R_CNG + c), bias=vcol(R_CNB + c)), r=[tk, "VEC"], w=[f"CAT{c}"]))

        QS, SGK, LF, LC, _, O32 = (TS[:, i, :] for i in range(6))
        XINF = XIN[:, :, :].rearrange("p a b -> p (a b)")
        PB = [
            dict(BTQ=BT[:, 0:2, :], VB=VB, KEB=KEB, KES=KES, DBL=DBL, GS=TS[:, 4, :]),
            dict(BTQ=XINF[:, 544:1088].bitcast(BF16).rearrange("p (a b) -> p a b", a=2),
                 VB=XINF[:, 1088:1664].bitcast(BF16).rearrange("p (a b) -> p a b", a=9),
                 KEB=XINF[:, 1664:2240].bitcast(BF16).rearrange("p (a b) -> p a b", a=9),
                 KES=XINF[:, 2240:2368].bitcast(BF16).rearrange("p (a b) -> p a b", a=2),
                 DBL=XINF[:, 2368:2384], GS=XINF[:, 0:544]),
        ]
        AK = (["BT0_1", "BT1_1", "DBL_1", "GS_1", "KES0_1", "KES1_1"] + [f"VB{j}_1" for j in range(9)] + [f"KEB{j}_1" for j in range(8)])

        def fence(keys):
            S.op("act", lambda e: e.activation(out=DBL[:, 15:16], in_=DER[:, 56:57], func=AF.Identity), r=["EPSK"], w=list(keys) + ["DBLx"])
        fence(["XIN0", "XIN1"] + AK)

        def phaseA(h):
            par = h % 2
            b = par
            B_ = PB[par]
            BTQ, VBp, KEBp, KESp, DBLp, GSp = B_["BTQ"], B_["VB"], B_["KEB"], B_["KES"], B_["DBL"], B_["GS"]
            kx = f"_{par}"
            for s_ in range(2):
                S.op("sp", lambda e, s_=s_: e.dma_start(out=SS0[:, b, s_, :], in_=sh[2 * p + s_, h, :, :]), w=[f"SS0_{b}"], dma=f"ss0_{b}")
            S.op("act", lambda e: e.activation(out=SS0B[:, b, :, :], in_=SS0[:, b, :, :], func=AF.Identity), r=[f"SS0_{b}"], w=[f"SS0B_{b}"])
            for (dst, key, func, brow, silu) in ((QS, "T0", AF.Sigmoid, 16, True), (SGK, "T1", AF.Sigmoid, 24, False),
                                                 (BT[:, 3, :], "BT3", AF.Identity, 32, False), (GSp, "GS" + kx, AF.Sigmoid, 40, True)):
                wv, kw = wnext(16, U_IN + brow + h)
                for (c0, n) in TT:
                    rg = ring_next()
                    o = PS[:, rg, 0:n]
                    k = f"ps{rg}"
                    for half in range(2):
                        def fn(e, o=o, wv=wv, c0=c0, n=n, half=half):
                            ins = None
                            for kc in range(8 * half, 8 * half + 8):
                                ins = e.matmul(o, lhsT=wv[:, kc, :], rhs=XTB[:, kc, c0:c0 + n], start=(kc == 0), stop=(kc == 15))
                            return ins
                        S.op("pe", fn, r=[kw] + XK, w=[k])
                        if half == 0:
                            yield
                    S.op("act", lambda e, o=o, c0=c0, n=n, dst=dst, func=func, brow=brow: e.activation(out=dst[:, c0:c0 + n], in_=o, func=func, bias=vcol(R_BIN + brow + h)), r=[k, "VEC"], w=[key])
                    if silu:
                        S.op("dve", lambda e, o=o, c0=c0, n=n, dst=dst, brow=brow: e.scalar_tensor_tensor(out=dst[:, c0:c0 + n], in0=o, scalar=vcol(R_BIN + brow + h), in1=dst[:, c0:c0 + n], op0=ALU.add, op1=ALU.mult), r=[k, key, "VEC"], w=[key])
                    yield
            S.op("act", lambda e: e.activation(out=LF, in_=SGK, func=AF.Ln, scale=DER[:, 8 + h:9 + h], bias=DER[:, h:h + 1]), r=["T1", "DER"], w=["T2"])
            S.op("act", lambda e: e.activation(out=SGK, in_=SGK, func=AF.Identity, scale=DER[:, 16 + h:17 + h], bias=DER[:, 8 + h:9 + h]), r=["T1", "DER"], w=["T1"])
            S.op("dve", lambda e: e.tensor_tensor_scan(out=LC, data0=CON[:, C_RMASK:C_RMASK + T], data1=LF, initial=0.0, op0=ALU.mult, op1=ALU.add), r=["T2", "CON"], w=["T3"])
            yield
            S.op("act", lambda e: e.activation(out=DBLp[:, 0:8], in_=LC[:, 0:TM].rearrange("p (j t) -> p j t", t=64)[:, :, 63], func=AF.Exp), r=["T3"], w=["DBL" + kx])
            S.op("act", lambda e: e.activation(out=DBLp[:, 8:10], in_=LC[:, TM:T].rearrange("p (j t) -> p j t", t=16)[:, :, 15], func=AF.Exp), r=["T3"], w=["DBL" + kx])
            S.op("act", lambda e: e.activation(out=LF, in_=LC, func=AF.Exp, scale=-1.0), r=["T3"], w=["T2"])
            S.op("act", lambda e: e.activation(out=LC, in_=LC, func=AF.Exp), r=["T3"], w=["T3"])
            yield
            S.op("dve", lambda e: e.tensor_tensor(out=BTQ[:, 0, :], in0=QS, in1=LC, op=ALU.mult), r=["T0", "T3"], w=["BT0" + kx])
            S.op("dve", lambda e: e.tensor_tensor(out=LF, in0=SGK, in1=LF, op=ALU.mult), r=["T1", "T2"], w=["T2"])
            S.op("act", lambda e: e.activation(out=BTQ[:, 1, :], in_=LF, func=AF.Identity), r=["T2"], w=["BT1" + kx])
            yield
            S.op("dve", lambda e: e.tensor_tensor(out=BT[:, 2, 0:TM].rearrange("p (j t) -> p j t", t=64), in0=LF[:, 0:TM].rearrange("p (j t) -> p j t", t=64),
                                                  in1=DBLp[:, 0:8].unsqueeze(2).to_broadcast([128, 8, 64]), op=ALU.mult), r=["T2", "DBL" + kx], w=["BT2"])
            S.op("dve", lambda e: e.tensor_tensor(out=BT[:, 2, TM:T].rearrange("p (j t) -> p j t", t=16), in0=LF[:, TM:T].rearrange("p (j t) -> p j t", t=16),
                                                  in1=DBLp[:, 8:10].unsqueeze(2).to_broadcast([128, 2, 16]), op=ALU.mult), r=["T2", "DBL" + kx], w=["BT2"])
            yield
            bk, k = transposes([(BT[:, 3, 64 * j:64 * j + 64], 128, 64, j * 128) for j in range(8)], IDB, BF16, ["BT3", "IDB"])
            copy_op(ev_eng(), VBp[0:64, 0:8, :], bk[0:64, 0:1024].rearrange("p (a b) -> p a b", a=8), r=[k], w=[f"VB{j}{kx}" for j in range(8)])
            yield
            bk, k = transposes([(BT[:, 2, 64 * j:64 * j + 64], 128, 64, j * 128) for j in range(8)], IDB, BF16, ["BT2", "IDB"])
            copy_op(ev_eng(), KEBp[0:64, 0:8, :], bk[0:64, 0:1024].rearrange("p (a b) -> p a b", a=8), r=[k], w=[f"KEB{j}{kx}" for j in range(8)])
            yield
            bk, k = transposes([(BT[:, 3, TM:T], 128, 32, 0), (BT[:, 2, TM:T], 128, 32, 128)], IDB, BF16, ["BT3", "BT2", "IDB"])
            copy_op("dve", VBp[0:32, 8, :], bk[0:32, 0:128], r=[k], w=["VB8" + kx])
            for s_ in range(2):
                S.op("dve", lambda e, bk=bk, s_=s_: e.tensor_scalar(out=KESp[0:32, s_, :], in0=bk[0:32, 128:256], scalar1=CON[0:32, C_SEQM + s_:C_SEQM + s_ + 1], scalar2=None, op0=ALU.mult), r=[k, "CON"], w=[f"KES{s_}{kx}"])
            yield

        def phaseB(h):
            par = h % 2
            b = par
            B_ = PB[par]
            BTQ, VBp, KEBp, KESp, DBLp, GSp = B_["BTQ"], B_["VB"], B_["KEB"], B_["KES"], B_["DBL"], B_["GS"]
            kx = f"_{par}"
            for j in range(9):
                sl = j % 2
                if j < 8:
                    c0, n = 64 * j, 64
                    mask = CON[0:64, C_CAUS:C_CAUS + 64]
                else:
                    c0, n = TM, 32
                    mask = CON[0:32, C_SMASK:C_SMASK + 32]
                rg = ring_next()
                sc = PS[0:n, rg, 0:n]
                mm_group(sc, [(BTQ[:, 1, c0:c0 + n], BTQ[:, 0, c0:c0 + n])], r=["BT0" + kx, "BT1" + kx], w=[f"ps{rg}"])
                scm = SCM[0:n, sl, 0:n]
                S.op("dve", lambda e, sc=sc, scm=scm, mask=mask: e.tensor_tensor(out=scm, in0=sc, in1=mask, op=ALU.mult), r=[f"ps{rg}", "CON"], w=[f"SCM{sl}"])
                yield
                rg = ring_next()
                ops_ = PS[:, rg, 0:n]
                pk = f"ps{rg}"
                rg2 = ring_next()
                pk2 = f"ps{rg2}"
                if j < 8:
                    su = PS[:, rg2, 0:128]
                    mm_group(su, [(KEBp[0:64, j, :], VBp[0:64, j, :])], r=[f"VB{j}{kx}", f"KEB{j}{kx}"], w=[pk2])
                    s32src, k32src = (S32[:, h, :], f"S32_{h}") if j == 0 else (SPP[:, (j - 1) % 2, :], f"SPP{(j - 1) % 2}")
                    s32dst, k32dst = (S32[:, h, :], f"S32_{h}") if j == 7 else (SPP[:, j % 2, :], f"SPP{j % 2}")
                    sbfsrc, kbfsrc = (SBF[:, h, :], f"SBF_{h}") if j == 0 else (SBP[:, (j - 1) % 2, :], f"SBP{(j - 1) % 2}")
                    sbfdst, kbfdst = (SBF[:, h, :], f"SBF_{h}") if j == 7 else (SBP[:, j % 2, :], f"SBP{j % 2}")
                    mm_group(ops_, [(VBp[0:n, j, :], scm), (sbfsrc, BTQ[:, 0, c0:c0 + n])], r=[f"VB{j}{kx}", f"SCM{sl}", kbfsrc, "BT0" + kx], w=[pk])
                    S.op("act", lambda e, ops_=ops_, c0=c0, n=n: e.activation(out=O32[:, c0:c0 + n], in_=ops_, func=AF.Identity), r=[pk], w=["T5"])
                    S.op("dve", lambda e, su=su, j=j, s32src=s32src, s32dst=s32dst: e.scalar_tensor_tensor(out=s32dst, in0=s32src, scalar=DBLp[:, j:j + 1], in1=su, op0=ALU.mult, op1=ALU.add), r=[pk2, "DBL" + kx, k32src], w=[k32dst])
                    S.op("act", lambda e, s32dst=s32dst, sbfdst=sbfdst: e.activation(out=sbfdst, in_=s32dst, func=AF.Identity), r=[k32dst], w=[kbfdst])
                else:
                    def fn(e, ops_=ops_, scm=scm):
                        e.matmul(ops_, lhsT=VBp[0:32, 8, :], rhs=scm, start=True, stop=False)
                        e.matmul(ops_[:, 0:16], lhsT=SS0B[:, b, 0, :], rhs=BTQ[:, 0, TM:TM + 16], start=False, stop=False)
                        return e.matmul(ops_[:, 16:32], lhsT=SS0B[:, b, 1, :], rhs=BTQ[:, 0, TM + 16:TM + 32], start=False, stop=True)
                    S.op("pe", fn, r=["VB8" + kx, f"SCM{sl}", f"SS0B_{b}", "BT0" + kx], w=[pk])

                    def fn2(e, rg2=rg2):
                        e.matmul(PS[:, rg2, 0:128], lhsT=KESp[0:32, 0, :], rhs=VBp[0:32, 8, :], start=True, stop=True)
                        return e.matmul(PS[:, rg2, 128:256], lhsT=KESp[0:32, 1, :], rhs=VBp[0:32, 8, :], start=True, stop=True)
                    S.op("pe", fn2, r=["VB8" + kx, "KES0" + kx, "KES1" + kx], w=[pk2])
                    S.op("act", lambda e, ops_=ops_, c0=c0, n=n: e.activation(out=O32[:, c0:c0 + n], in_=ops_, func=AF.Identity), r=[pk], w=["T5"])
                    for s_ in range(2):
                        S.op("dve", lambda e, rg2=rg2, s_=s_: e.scalar_tensor_tensor(out=SS1[:, b, s_, :], in0=SS0[:, b, s_, :], scalar=DBLp[:, 8 + s_:9 + s_], in1=PS[:, rg2, 128 * s_:128 + 128 * s_], op0=ALU.mult, op1=ALU.add), r=[pk2, "DBL" + kx, f"SS0_{b}"], w=[f"SS1_{b}"])
                    for s_ in range(2):
                        S.op("sp", lambda e, s_=s_: e.dma_start(out=hs_o[2 * p + s_, h, :, :], in_=SS1[:, b, s_, :]), r=[f"SS1_{b}"], dma=f"ss1_{b}")
                    out_dmas.append(f"ss1_{b}")
                yield
            if p == 1:
                S.op("sp", lambda e: e.dma_start(out=hm[h, :, :], in_=S32[:, h, :]), r=[f"S32_{h}"], dma="hmo")
                out_dmas.append("hmo")
            S.op("act", lambda e: e.activation(out=SCR[:, 0, :], in_=O32, func=AF.Square), r=["T5"], w=["SCR0"])
            for (c0, n) in TT:
                rg = ring_next()
                ssp = PS[:, rg, 0:n]
                mm_group(ssp, [(ONH[:, :], SCR[:, 0, c0:c0 + n])], r=["ONH", "SCR0"], w=[f"ps{rg}"])
                S.op("act", lambda e, ssp=ssp, n=n: e.activation(out=LNS[:, 2, 0:n], in_=ssp, func=AF.Ln, bias=DER[:, 56:57]), r=[f"ps{rg}", "EPSK"], w=["LNS2"])
                S.op("act", lambda e, n=n: e.activation(out=LNS[:, 2, 0:n], in_=LNS[:, 2, 0:n], func=AF.Exp, scale=-0.5), r=["LNS2"], w=["LNS2"])
                S.op("dve", lambda e, c0=c0, n=n: e.scalar_tensor_tensor(out=O32[:, c0:c0 + n], in0=O32[:, c0:c0 + n], scalar=vcol(R_HNG), in1=LNS[:, 2, 0:n], op0=ALU.mult, op1=ALU.mult), r=["T5", "LNS2", "VEC"], w=["T5"])
                S.op("dve", lambda e, c0=c0, n=n: e.tensor_tensor(out=CAT[:, 8 + h, c0:c0 + n], in0=O32[:, c0:c0 + n], in1=GSp[:, c0:c0 + n], op=ALU.mult), r=["T5", "GS" + kx], w=[f"CAT{8 + h}"])
                yield

        def merge(gens):
            gens = [g for g in gens if g is not None]
            while gens:
                for g in list(gens):
                    try:
                        next(g)
                    except StopIteration:
                        gens.remove(g)
        wlook[0] = NSLOT - 2
        merge([phaseA(0), conv_chunk(0)])
        for h in range(NH):
            merge([phaseB(h), phaseA(h + 1) if h + 1 < NH else None, conv_chunk(h + 1) if h + 1 < NH else None])
        wlook[0] = NSLOT
        conv_ln()
        fence(["XIN0", "XIN1"] + AK)

        if KPASS < 4:
            return
        CK = [f"CAT{k}" for k in range(16)]
        for dc in range(16):
            wv, kw = wnext(16, U_OUT + dc)
            for (c0, n) in TT:
                o, k = proj(wv, kw, 16, lambda kc, c0, n: CAT[:, kc, c0:c0 + n], CK, c0, n)
                S.op("dve", lambda e, o=o, c0=c0, n=n, dc=dc: e.scalar_tensor_tensor(out=XT32[:, dc, c0:c0 + n], in0=XT32[:, dc, c0:c0 + n], scalar=ALPHA, in1=o, op0=ALU.mult, op1=ALU.add), r=[k, f"X{dc}"], w=[f"X{dc}"])

        def ln1_out(c, t, tk):
            S.op("act", lambda e: e.activation(out=XTB[:, c, :], in_=t, func=AF.Identity, scale=vcol(R_L1G + c), bias=vcol(R_L1B + c)), r=[tk, "VEC"], w=[f"B{c}"])
            S.op("act", lambda e: e.activation(out=XT32[:, c, :], in_=t, func=AF.Identity, scale=DER[:, 24 + c:25 + c], bias=DER[:, 40 + c:41 + c]), r=[tk, "DER"], w=[f"X{c}"])
        layer_norm(lambda c: XT32[:, c, :], [f"X{c}" for c in range(16)], 16, OND, "OND", ln1_out)

        if KPASS < 5:
            return
        for g in range(11):
            ab = g % 2
            for f in range(4):
                wg, kg_ = wnext(16, U_GATE + 4 * g + f)
                gps = []
                for (c0, n) in TT:
                    o, k = proj(wg, kg_, 16, rhsx, XK, c0, n)
                    sgs = TS[:, len(gps), 0:n]
                    tk = f"T{len(gps)}"
                    S.op("act", lambda e, o=o, sgs=sgs: e.activation(out=sgs, in_=o, func=AF.Silu), r=[k], w=[tk])
                    gps.append((sgs, tk))
                wu, ku = wnext(16, U_UP + 4 * g + f)
                for i, (c0, n) in enumerate(TT):
                    o, k = proj(wu, ku, 16, rhsx, XK, c0, n)
                    sgs, tk = gps[i]
                    ak = f"CP{(ab * 4 + f) // 2}"
                    S.op("dve", lambda e, o=o, sgs=sgs, c0=c0, n=n, ab=ab, f=f: e.tensor_tensor(out=ACTB[:, ab, f, c0:c0 + n], in0=sgs, in1=o, op=ALU.mult), r=[k, tk], w=[ak])
            aks = [f"CP{ab * 2}", f"CP{ab * 2 + 1}"]
            for dc in range(16):
                if dc % 4 == 0:
                    wv, kw = wnext(4, U_DOWN + 4 * g + dc // 4)
                sub = ((dc % 4) * 128, (dc % 4) * 128 + 128)
                for (c0, n) in TT:
                    o, k = proj(wv, kw, 4, lambda kc, c0, n, ab=ab: ACTB[:, ab, kc, c0:c0 + n], aks, c0, n, wsub=sub)
                    S.op("dve", lambda e, o=o, c0=c0, n=n, dc=dc: e.tensor_tensor(out=XT32[:, dc, c0:c0 + n], in0=XT32[:, dc, c0:c0 + n], in1=o, op=ALU.add), r=[k, f"X{dc}"], w=[f"X{dc}"])

        if KPASS < 6:
            return
        def ln2_out(c, t, tk):
            S.op("act", lambda e: e.activation(out=XT32[:, c, :], in_=t, func=AF.Identity, scale=vcol(R_L2G + c), bias=vcol(R_L2B + c)), r=[tk, "VEC"], w=[f"X{c}"])
        layer_norm(lambda c: XT32[:, c, :], [f"X{c}" for c in range(16)], 16, OND, "OND", ln2_out)

        for ti in range(5):
            c0, n = (ti * 128, 128) if ti < 4 else (TM, 32)
            b = xbuf[0]
            xbuf[0] ^= 1
            for g in range(4):
                rg = ring_next()

                def fn(e, g=g, rg=rg, c0=c0, n=n):
                    ins = None
                    for q in range(4):
                        ins = e.transpose(PS[0:n, rg, q * 128:(q + 1) * 128], XT32[:, 4 * g + q, c0:c0 + n], ID32)
                    return ins
                S.op("pe", fn, r=[f"X{4 * g + q}" for q in range(4)] + ["CON"], w=[f"ps{rg}"])
                copy_op(ev_eng(), XIN[0:n, b, g * 512:(g + 1) * 512], PS[0:n, rg, :], r=[f"ps{rg}"], w=[f"XIN{b}"])
            dst = ym[p * TM + c0: p * TM + c0 + n, :] if ti < 4 else ys[p * 32:(p + 1) * 32, :]
            S.op("sp", lambda e, b=b, n=n, dst=dst: e.dma_start(out=dst, in_=XIN[0:n, b, :]), r=[f"XIN{b}"], dma=f"xin{b}")
            out_dmas.append(f"xin{b}")

    def layer_norm(src, skeys, nch, ones, okey, out_fn):
        banks = [ring_next() for _ in range(4)]
        bkeys = [f"ps{b_}" for b_ in banks]
        for c in range(nch):
            sl = c % 2
            cb = SCR[:, 2 * sl, :]
            sq = SCR[:, 2 * sl + 1, :]
            S.op("act", lambda e, c=c, cb=cb: e.activation(out=cb, in_=src(c), func=AF.Identity), r=[skeys[c]], w=[f"SCR{2 * sl}"])
            S.op("act", lambda e, c=c, sq=sq: e.activation(out=sq, in_=src(c), func=AF.Square), r=[skeys[c]], w=[f"SCR{2 * sl + 1}"])

            def fn(e, c=c, cb=cb, sq=sq):
                ins = None
                for ti, (c0, n) in enumerate(TT):
                    e.matmul(PS[:, banks[2 * ti], 0:n], lhsT=ones[:, :], rhs=cb[:, c0:c0 + n], start=(c == 0), stop=(c == nch - 1))
                    ins = e.matmul(PS[:, banks[2 * ti + 1], 0:n], lhsT=ones[:, :], rhs=sq[:, c0:c0 + n], start=(c == 0), stop=(c == nch - 1))
                return ins
            S.op("pe", fn, r=[okey, f"SCR{2 * sl}", f"SCR{2 * sl + 1}"], w=bkeys)
        for ti, (c0, n) in enumerate(TT):
            MEAN, VAR, RSTD = LNS[:, 0, c0:c0 + n], LNS[:, 1, c0:c0 + n], LNS[:, 2, c0:c0 + n]
            mps, qps = PS[:, banks[2 * ti], 0:n], PS[:, banks[2 * ti + 1], 0:n]
            S.op("act", lambda e, mps=mps, MEAN=MEAN: e.activation(out=MEAN, in_=mps, func=AF.Identity), r=[bkeys[2 * ti]], w=["LNS0"])
            S.op("dve", lambda e, MEAN=MEAN, VAR=VAR: e.tensor_tensor(out=VAR, in0=MEAN, in1=MEAN, op=ALU.mult), r=["LNS0"], w=["LNS1"])
            S.op("dve", lambda e, qps=qps, VAR=VAR: e.tensor_tensor(out=VAR, in0=qps, in1=VAR, op=ALU.subtract), r=[bkeys[2 * ti + 1], "LNS1"], w=["LNS1"])
            S.op("act", lambda e, VAR=VAR, RSTD=RSTD: e.activation(out=RSTD, in_=VAR, func=AF.Ln, bias=DER[:, 56:57]), r=["LNS1", "EPSK"], w=["LNS2"])
            S.op("act", lambda e, RSTD=RSTD: e.activation(out=RSTD, in_=RSTD, func=AF.Exp, scale=-0.5), r=["LNS2"], w=["LNS2"])
        for c in range(nch):
            t = TS[:, 6 + (c % 2), :]
            tk = f"T{6 + (c % 2)}"
            S.op("dve", lambda e, c=c, t=t: e.tensor_tensor(out=t, in0=src(c), in1=LNS[:, 0, :], op=ALU.subtract), r=[skeys[c], "LNS0"], w=[tk])
            S.op("dve", lambda e, t=t: e.tensor_tensor(out=t, in0=t, in1=LNS[:, 2, :], op=ALU.mult), r=[tk, "LNS2"], w=[tk])
            out_fn(c, t, tk)

    tp_ctr = [0]
    out_dmas = []
    import os
    STAGE = int(os.environ.get("KSTAGE", "9"))
    if STAGE >= 1:
        emit_prefix()
    if STAGE >= 2:
        emit_pass(0)
    if STAGE >= 9:
        emit_pass(1)
    fin = S.op("sp", None, r=[], w=[])
    fin.dmadeps = {st: 16 * S.dma_cnt[st] for st in set(out_dmas)}
    S.finalize()

    prog = {e: es.enter_context(nc.semaphore(f"prog_{e}")) for e in S.ENG}
    dmasems = {k: es.enter_context(nc.semaphore(f"dma_{k}")) for k in S.dma_cnt}
    with nc.Block() as block:
        @block.tensor
        def _(eng):
            S.replay("pe", eng, prog, dmasems)

        @block.scalar
        def _(eng):
            S.replay("act", eng, prog, dmasems)

        @block.vector
        def _(eng):
            S.replay("dve", eng, prog, dmasems)

        @block.gpsimd
        def _(eng):
            S.replay("pool", eng, prog, dmasems)

        @block.sync
        def _(eng):
            S.replay("sp", eng, prog, dmasems)
    es.close()
    return nc, wrec


def _consts():
    c = np.zeros((128, NCONST), np.float32)
    c[:, C_ID:C_ID + 128] = np.eye(128, dtype=np.float32)
    s = np.arange(64)
    c[0:64, C_CAUS:C_CAUS + 64] = (s[:, None] <= s[None, :]).astype(np.float32)
    s = np.arange(32)
    c[0:32, C_SMASK:C_SMASK + 32] = ((s[:, None] <= s[None, :]) & ((s[:, None] // 16) == (s[None, :] // 16))).astype(np.float32)
    c[0:16, C_SEQM] = 1.0
    c[16:32, C_SEQM + 1] = 1.0
    rm = np.ones(T, np.float32)
    rm[0:TM:64] = 0.0
    rm[TM:T:16] = 0.0
    c[:, C_RMASK:C_RMASK + T] = rm[None, :]
    c[:, C_PRM:C_PRM + 2] = 1.0
    return c


def _vecs(b_in, w_dw, b_dw, cng, cnb, lbs, hng, l1g, l1b, l2g, l2b):
    v = np.zeros((NVEC, 128), np.float32)
    v[R_BIN:R_BIN + 48] = b_in.reshape(48, 128)
    v[R_WDW:R_WDW + 248] = w_dw.reshape(31 * 8, 128)
    v[R_BDW:R_BDW + 8] = b_dw.reshape(8, 128)
    v[R_CNG:R_CNG + 8] = cng.reshape(8, 128)
    v[R_CNB:R_CNB + 8] = cnb.reshape(8, 128)
    v[R_LB0:R_LB0 + 8] = lbs[0].reshape(8, 128)
    v[R_LB1:R_LB1 + 8] = lbs[1].reshape(8, 128)
    v[R_HNG] = hng.reshape(128)
    v[R_L1G:R_L1G + 16] = l1g.reshape(16, 128)
    v[R_L1B:R_L1B + 16] = l1b.reshape(16, 128)
    v[R_L2G:R_L2G + 16] = l2g.reshape(16, 128)
    v[R_L2B:R_L2B + 16] = l2b.reshape(16, 128)
    return v


_NC_CACHE = {}


def kernel(x_prompt, x_sample, cache_conv, state_hgrn, w_in, b_in, w_dw, b_dw, conv_norm_g, conv_norm_b,
           hg_lower_bounds, hg_norm_g, w_out, ln1_g, ln1_b, w_gate, w_up, w_down, ln2_g, ln2_b):
    f = lambda a: np.ascontiguousarray(np.asarray(a, dtype=np.float32))
    x_prompt, x_sample, cache_conv, state_hgrn = f(x_prompt), f(x_sample), f(cache_conv), f(state_hgrn)
    vec = _vecs(f(b_in)[0], f(w_dw)[0], f(b_dw)[0], f(conv_norm_g)[0], f(conv_norm_b)[0], f(hg_lower_bounds),
                f(hg_norm_g)[0], f(ln1_g)[0], f(ln1_b)[0], f(ln2_g)[0], f(ln2_b)[0])
    con = _consts()
    def colunits(w2d):
        C = w2d.shape[1]
        return np.ascontiguousarray(w2d.reshape(16, 128, C // 128, 128).transpose(2, 1, 0, 3)).reshape(C // 128, 128, 2048)
    wd = f(w_down)[0].reshape(11, 4, 128, 4, 512)
    wdu = np.ascontiguousarray(wd.transpose(0, 3, 2, 1, 4)).reshape(44, 128, 2048)
    wall = np.concatenate([colunits(f(w_in)[0]), colunits(f(w_out)[0]), colunits(f(w_gate)[0]), colunits(f(w_up)[0]), wdu], axis=0)
    assert wall.shape == (NUNIT, 128, 2048)
    shared = {"vecs": vec, "consts": con, "wall": wall}
    in_maps = []
    for core in range(8):
        b, half = core // 2, core % 2
        xm = x_prompt[b, half * 1024:(half + 1) * 1024]
        if half == 0:
            xh = np.zeros((32, D), np.float32)
            xp = np.zeros((1024, D), np.float32)
        else:
            xh = x_prompt[b, 1024 - 32:1024]
            xp = x_prompt[b, 0:1024]
        m = dict(shared)
        m.update({"xm": f(xm), "xh": f(xh), "xp": f(xp), "xs": f(x_sample[core * 4:(core + 1) * 4].reshape(64, D)),
                  "flag": np.full((128, 1), float(half), np.float32), "cc": f(cache_conv[0, core * 4:(core + 1) * 4]),
                  "sh": f(state_hgrn[0, core * 4:(core + 1) * 4])})
        in_maps.append(m)
    if "nc" not in _NC_CACHE:
        _, order = build_nc(None)
        _NC_CACHE["nc"], _ = build_nc(order)
    res = run_bass_kernel_spmd(_NC_CACHE["nc"], in_maps, core_ids=list(range(8)))
    R = res.results
    yp = np.zeros((4, 2048, D), np.float32)
    ysm = np.zeros((32, 16, D), np.float32)
    cp = np.zeros((1, 4, 30, 1024), np.float32)
    hp = np.zeros((1, 4, NH, 128, 128), np.float32)
    csm = np.zeros((1, 32, 30, 1024), np.float32)
    hsm = np.zeros((1, 32, NH, 128, 128), np.float32)
    for core in range(8):
        b, half = core // 2, core % 2
        r = R[core]
        yp[b, half * 1024:(half + 1) * 1024] = r["ym"]
        ysm[core * 4:(core + 1) * 4] = r["ys"].reshape(4, 16, D)
        csm[0, core * 4:(core + 1) * 4] = r["cs"]
        hsm[0, core * 4:(core + 1) * 4] = r["hs"]
        if half == 1:
            cp[0, b] = r["cm"]
            hp[0, b] = r["hm"]
    return (yp, ysm, cp, hp, csm, hsm)
```

```python
import numpy as np
from contextlib import ExitStack
import concourse.bass as bass
import concourse.mybir as mybir
from concourse.bass_utils import run_bass_kernel_spmd

F32 = mybir.dt.float32
BF16 = mybir.dt.bfloat16
AF = mybir.ActivationFunctionType
ALU = mybir.AluOpType

D = 2048
DFF = 5632
NH = 8
TM = 512
NS = 2
T = TM + 16 * NS
TT = [(0, 272), (272, 272)]
ALPHA = 2.0 ** 0.25
EPS = 1e-5
NSLOT = 6
NUNIT = 196
UEW = 640
SMP0 = 30 + TM
SAME_ENG_SYNC = True
DEBUG = False
USE_SCAN = True
HG_PIPE = True

R_BIN, R_WDW, R_BDW, R_CNG, R_CNB, R_LB0, R_LB1, R_HNG = 0, 48, 296, 304, 312, 320, 328, 336
R_L1G, R_L1B, R_L2G, R_L2B = 337, 353, 369, 385
NVEC = 512
C_ID, C_CAUS, C_SMASK, C_SEQM, C_RMASK, C_PRM, NCONST = 0, 128, 192, 224, 226, 770, 772


class Op:
    __slots__ = ("eng", "fn", "deps", "mark", "val", "dma", "dmaval", "dmadeps")


class Sched:
    ENG = ("pe", "act", "dve", "pool", "sp")

    def __init__(self):
        self.q = {e: [] for e in self.ENG}
        self.lastw = {}
        self.rd = {}
        self.dma_cnt = {}

    def op(self, eng, fn, r=(), w=(), dma=None):
        o = Op()
        o.eng, o.fn, o.deps, o.mark, o.val, o.dma, o.dmaval = eng, fn, [], False, 0, dma, 0
        deps = {}
        for k in list(r) + list(w):
            lw = self.lastw.get(k)
            if lw is not None:
                deps[id(lw)] = lw
        for k in w:
            for rr in self.rd.get(k, {}).values():
                deps[id(rr)] = rr
        o.deps = [d for d in deps.values() if d.dma is None]
        o.dmadeps = {d.dma: 16 * self.dma_cnt[d.dma] for d in deps.values() if d.dma is not None}
        for k in r:
            d = self.rd.setdefault(k, {})
            d[eng if dma is None else ("dma", id(o))] = o
        for k in w:
            self.lastw[k] = o
            self.rd[k] = {}
        if dma is not None:
            self.dma_cnt[dma] = self.dma_cnt.get(dma, 0) + 1
            o.dmaval = 16 * self.dma_cnt[dma]
        self.q[eng].append(o)
        return o

    def finalize(self):
        for e in self.ENG:
            for o in self.q[e]:
                for d in o.deps:
                    if d.dma is None:
                        d.mark = True
        for e in self.ENG:
            c = 0
            for o in self.q[e]:
                if o.mark and o.dma is None:
                    c += 1
                    o.val = c

    def replay(self, ename, eng, prog, dmasems):
        waited = {}
        for o in self.q[ename]:
            wl = [(("d", st), dmasems[st], v) for st, v in o.dmadeps.items()]
            for d in o.deps:
                if d.eng == ename and (ename == "pe" or not SAME_ENG_SYNC):
                    continue
                wl.append((("p", d.eng), prog[d.eng], d.val))
            for key, sem, val in wl:
                if waited.get(key, 0) >= val:
                    continue
                eng.wait_ge(sem, val)
                waited[key] = val
            if o.fn is None:
                continue
            ins = o.fn(eng)
            if o.dma is not None:
                ins.then_inc(dmasems[o.dma], 16)
            elif o.mark:
                ins.then_inc(prog[ename], 1)


def build_nc(order=None):
    dry = order is None
    import os
    SUB = int(os.environ.get("KSUB", "9"))
    nc = bass.Bass("TRN2", target_bir_lowering=False)
    S = Sched()
    es = ExitStack()

    def din(name, shape):
        return nc.dram_tensor(name, shape, F32, kind="ExternalInput").ap()

    def dout(name, shape):
        return nc.dram_tensor(name, shape, F32, kind="ExternalOutput").ap()

    xm = din("xm", [1024, D]); xh = din("xh", [32, D]); xp = din("xp", [1024, D]); xs = din("xs", [64, D])
    flag_d = din("flag", [128, 1]); cc = din("cc", [4, 30, 1024]); sh = din("sh", [4, NH, 128, 128])
    vecs = din("vecs", [NVEC, 128]); consts_d = din("consts", [128, NCONST])
    wall = din("wall", [NUNIT, 128, 2048])
    ym = dout("ym", [1024, D]); ys = dout("ys", [64, D]); cm = dout("cm", [30, 1024])
    hm = dout("hm", [NH, 128, 128]); cs_o = dout("cs", [4, 30, 1024]); hs_o = dout("hs", [4, NH, 128, 128])

    def sb(name, shape, dt=F32):
        return es.enter_context(nc.sbuf_tensor(name, shape, dt))

    RX = sb("RX", [128, 16 * T]); RB = sb("RB", [128, 16 * T], BF16); RC = sb("RC", [128, 16 * T])
    CAT = sb("CAT", [128, 16, T], BF16); SCR = sb("SCR", [128, 4, T], BF16)
    UE = sb("UE", [128, 2, UEW]); BT = sb("BT", [128, 4, T], BF16)
    VB = sb("VB", [128, 9, 128], BF16); KEB = sb("KEB", [128, 9, 128], BF16); KES = sb("KES", [128, 2, 128], BF16)
    S32 = sb("S32", [128, NH, 128]); SBF = sb("SBF", [128, NH, 128], BF16)
    SS0 = sb("SS0", [128, 2, 2, 128]); SS0B = sb("SS0B", [128, 2, 2, 128], BF16); SS1 = sb("SS1", [128, 2, 2, 128])
    WR = sb("WR", [128, NSLOT, 2048], BF16)
    XIN = sb("XIN", [128, 2, D])
    CST = sb("CST", [128, 2, 384]); SPP = sb("SPP", [128, 2, 128]); SBP = sb("SBP", [128, 2, 128], BF16)
    CON = sb("CON", [128, NCONST]); IDB = sb("IDB", [128, 128], BF16)
    ONC = sb("ONC", [128, 128], BF16); OND = sb("OND", [128, 128], BF16); ONH = sb("ONH", [128, 128], BF16)
    VEC = sb("VEC", [128, NVEC]); DER = sb("DER", [128, 64]); FLG = sb("FLG", [128, 1])
    CACHE = sb("CACHE", [128, 8, 4, 30]); UH = sb("UH", [128, 2, 8, 30]); UHS = sb("UHS", [128, 2, 32])
    LNS = sb("LNS", [128, 3, T]); DBL = sb("DBL", [128, 16]); SCM = sb("SCM", [128, 2, 64], BF16)
    XHT = sb("XHT", [128, 16, 32], BF16)
    UEB = sb("UEB", [128, 2, UEW], BF16); DG = sb("DG", [128, 31, 128], BF16)
    PS = es.enter_context(nc.psum_tensor("PS", [128, 8, 512], F32))

    XT32 = RX[:, :].rearrange("p (a b) -> p a b", a=16)
    XTB = RB[:, :].rearrange("p (a b) -> p a b", a=16)
    CP = RC[:, 0:8 * T].rearrange("p (a b) -> p a b", a=8)
    TS = RC[:, 8 * T:16 * T].rearrange("p (a b) -> p a b", a=8)
    ACTB = RC[:, 0:4 * T].bitcast(BF16).rearrange("p (g f t) -> p g f t", g=2, f=4)
    XPT = RX[:, :].bitcast(BF16)[:, 0:16 * 1024].rearrange("p (a b) -> p a b", a=16)
    PT = RC[:, 0:3 * 1024].rearrange("p (a b) -> p a b", a=3)
    PTB = RC[:, 3 * 1024:5 * 1024].bitcast(BF16).rearrange("p (a b) -> p a b", a=4)
    PVB = RC[:, 5 * 1024:6 * 1024].bitcast(BF16).rearrange("p (a b) -> p a b", a=16)

    def vcol(r):
        return VEC[:, r:r + 1]

    ring = [0]

    def ring_next():
        r = ring[0]
        ring[0] = (r + 1) % 8
        return r

    tpi = [0]

    def tp_next():
        r = tpi[0]
        tpi[0] = (r + 1) % 4
        return r

    evt = [0]

    def ev_eng():
        evt[0] ^= 1
        return "act" if evt[0] else "dve"

    def copy_op(eng, out, in_, r, w):
        if eng == "act":
            return S.op("act", lambda e: e.activation(out=out, in_=in_, func=AF.Identity), r=r, w=w)
        return S.op(eng, lambda e: e.tensor_copy(out=out, in_=in_), r=r, w=w)

    wq = [] if dry else list(order)
    wrec = []
    wstate = {"issued": 0}

    def wissue(upto):
        while wstate["issued"] < min(upto, len(wq)):
            i = wstate["issued"]
            uid = wq[i]
            slot = i % NSLOT
            S.op("pool", lambda e, slot=slot, uid=uid: e.dma_start(out=WR[:, slot, :], in_=wall[uid, :, :]),
                 w=[f"W{slot}"], dma=f"W{slot}")
            wstate["issued"] += 1

    wcons = [0]
    wlook = [NSLOT]

    def wnext(a, uid):
        i = wcons[0]
        wcons[0] += 1
        wrec.append(uid)
        if dry:
            wq.append(uid)
            wissue(i + 1)
        else:
            assert wq[i] == uid, (i, wq[i], uid)
            wissue(i + wlook[0])
        slot = i % NSLOT
        return WR[:, slot, :].rearrange("p (a b) -> p a b", a=a), f"W{slot}"

    U_IN, U_OUT, U_GATE, U_UP, U_DOWN = 0, 48, 64, 108, 152

    def mm_group(out, pairs, r, w):
        def fn(e, out=out, pairs=pairs):
            n = len(pairs)
            ins = None
            for i, (l, rh) in enumerate(pairs):
                ins = e.matmul(out, lhsT=l, rhs=rh, start=(i == 0), stop=(i == n - 1))
            return ins
        return S.op("pe", fn, r=r, w=w)

    def proj(wv, wkey, nk, rhs_fn, rkeys, c0, n, wsub=None):
        rg = ring_next()
        out = PS[:, rg, 0:n]
        pairs = [((wv[:, kc, :] if wsub is None else wv[:, kc, wsub[0]:wsub[1]]), rhs_fn(kc, c0, n)) for kc in range(nk)]
        mm_group(out, pairs, r=[wkey] + rkeys, w=[f"ps{rg}"])
        return out, f"ps{rg}"

    def transposes(items, ident, dt, rkeys):
        rg = ring_next()
        bank = PS[:, rg, :] if dt == F32 else PS[:, rg, :].bitcast(BF16)

        def fn(e):
            ins = None
            for (in_ap, np_, nf, col) in items:
                ins = e.transpose(bank[0:nf, col:col + np_], in_ap, ident[0:np_, 0:np_])
            return ins
        S.op("pe", fn, r=rkeys, w=[f"ps{rg}"])
        return bank, f"ps{rg}"

    def cumsum_steps(src, bufa, bufb, ka, kb, ksrc, blk, nblk):
        v = lambda ap: ap.rearrange("p (j t) -> p j t", t=blk)
        cur, ck = src, ksrc
        dsts = [(bufa, ka), (bufb, kb)]
        i, sft = 0, 1
        while sft < blk:
            dst, dk = dsts[i % 2]
            S.op("dve", lambda e, cur=cur, dst=dst, sft=sft: e.tensor_tensor(out=v(dst)[:, :, 0:sft], in0=v(cur)[:, :, 0:sft], in1=v(cur)[:, :, 0:sft], op=ALU.max), r=[ck], w=[dk])
            S.op("dve", lambda e, cur=cur, dst=dst, sft=sft: e.tensor_tensor(out=v(dst)[:, :, sft:blk], in0=v(cur)[:, :, sft:blk], in1=v(cur)[:, :, 0:blk - sft], op=ALU.add), r=[ck], w=[dk])
            cur, ck = dst, dk
            i += 1
            sft *= 2
        return cur

    S.op("sp", lambda e: e.dma_start(out=CON[:, :], in_=consts_d[:, :]), w=["CON"], dma="cin")
    S.op("sp", lambda e: e.dma_start(out=FLG[:, :], in_=flag_d[:, :]), w=["FLG"], dma="fin")
    S.op("act", lambda e: e.activation(out=IDB[:, :], in_=CON[:, C_ID:C_ID + 128], func=AF.Identity), r=["CON"], w=["IDB"])
    S.op("pool", lambda e: e.memset(ONC[:, :], 1.0 / 1024.0), w=["ONC"])
    S.op("pool", lambda e: e.memset(OND[:, :], 1.0 / 2048.0), w=["OND"])
    S.op("pool", lambda e: e.memset(ONH[:, :], 1.0 / 128.0), w=["ONH"])
    S.op("pool", lambda e: e.memset(DER[:, 56:57], EPS), w=["EPSK"])
    S.op("pool", lambda e: e.memset(UE[:, :, :], 0.0), w=["UE0", "UE1"])
    ID32 = CON[:, C_ID:C_ID + 128]
    for i in range(4 if SUB >= 2 else 0):
        S.op("sp", lambda e, i=i: e.dma_start(out=XIN[:, i // 2, (i % 2) * 128:(i % 2) * 128 + 128], in_=vecs[i * 128:(i + 1) * 128, :]),
             w=[f"XIN{i // 2}"], dma=f"xin{i // 2}")
    bk, k = transposes([(XIN[:, i // 2, (i % 2) * 128:(i % 2) * 128 + 128], 128, 128, i * 128) for i in range(4)], ID32, F32, ["XIN0", "XIN1", "CON"])
    copy_op("dve", VEC[:, :], bk[:, 0:512], r=[k], w=["VEC"])
    if SUB >= 3:
      S.op("dve", lambda e: e.tensor_tensor(out=DER[:, 0:8], in0=VEC[:, R_LB0:R_LB0 + 8], in1=VEC[:, R_LB1:R_LB1 + 8], op=ALU.subtract), r=["VEC"], w=["DER"])
      S.op("act", lambda e: e.activation(out=DER[:, 0:8], in_=DER[:, 0:8], func=AF.Sigmoid), r=["DER"], w=["DER"])
      S.op("act", lambda e: e.activation(out=DER[:, 8:16], in_=DER[:, 0:8], func=AF.Identity, scale=-1.0, bias=1.0), r=["DER"], w=["DER"])
      S.op("act", lambda e: e.activation(out=DER[:, 16:24], in_=DER[:, 0:8], func=AF.Identity, scale=1.0, bias=-1.0), r=["DER"], w=["DER"])
      S.op("act", lambda e: e.activation(out=DER[:, 24:56], in_=VEC[:, R_L1G:R_L1G + 32], func=AF.Identity, scale=ALPHA), r=["VEC"], w=["DER"])
    for s in range(4 if SUB >= 4 else 0):
        S.op("sp", lambda e, s=s: e.dma_start(out=XIN[0:30, s % 2, 0:1024], in_=cc[s, :, :]), w=[f"XIN{s % 2}"], dma=f"xin{s % 2}")
        bk, k = transposes([(XIN[0:32, s % 2, c * 128:(c + 1) * 128], 32, 128, c * 32) for c in range(8)], ID32, F32, [f"XIN{s % 2}", "CON"])
        copy_op(ev_eng(), CACHE[:, :, s, :], bk[:, 0:256].rearrange("p (c j) -> p c j", c=8)[:, :, 0:30], r=[k], w=["CACHE"])

    xbuf = [0]

    def load_xT(src, nrows, dstb, dst32, col0, wkeys):
        b = xbuf[0]
        xbuf[0] ^= 1
        S.op("sp", lambda e: e.dma_start(out=XIN[0:nrows, b, :], in_=src), w=[f"XIN{b}"], dma=f"xin{b}")
        for g in range(4):
            rg = ring_next()

            def fn(e, g=g, rg=rg):
                ins = None
                for q in range(4):
                    kc = 4 * g + q
                    ins = e.transpose(PS[:, rg, q * 128:q * 128 + nrows], XIN[0:nrows, b, kc * 128:(kc + 1) * 128], ID32[0:nrows, 0:nrows])
                return ins
            S.op("pe", fn, r=[f"XIN{b}", "CON"], w=[f"ps{rg}"])
            src_ps = PS[:, rg, :].rearrange("p (q c) -> p q c", q=4)[:, :, 0:nrows]
            ce = ev_eng()
            copy_op(ce, dstb[:, 4 * g:4 * g + 4, col0:col0 + nrows], src_ps, r=[f"ps{rg}"], w=[wkeys[0] + str(kc) for kc in range(4 * g, 4 * g + 4)])
            if dst32 is not None:
                copy_op(ce, dst32[:, 4 * g:4 * g + 4, col0:col0 + nrows], src_ps, r=[f"ps{rg}"], w=[wkeys[1] + str(kc) for kc in range(4 * g, 4 * g + 4)])

    def emit_prefix():
        for ti in range(8):
            load_xT(xp[ti * 128:(ti + 1) * 128, :], 128, XPT, None, ti * 128, ["X"])
        XK = [f"X{k}" for k in range(16)]
        PSUB = int(os.environ.get("KPSUB", "9"))
        for h in range(NH if PSUB >= 2 else 0):
            wf, kf = wnext(16, U_IN + 24 + h)
            for (c0, n) in ((0, 512), (512, 512)):
                o, k = proj(wf, kf, 16, lambda kc, c0, n: XPT[:, kc, c0:c0 + n], XK, c0, n)
                S.op("act", lambda e, o=o, c0=c0, n=n, h=h: e.activation(out=PT[:, 0, c0:c0 + n], in_=o, func=AF.Sigmoid, bias=vcol(R_BIN + 24 + h)), r=[k, "VEC"], w=["T0"])
            wi, ki = wnext(16, U_IN + 32 + h)
            for (c0, n) in ((0, 512), (512, 512)):
                o, k = proj(wi, ki, 16, lambda kc, c0, n: XPT[:, kc, c0:c0 + n], XK, c0, n)
                S.op("act", lambda e, o=o, c0=c0, n=n, h=h: e.activation(out=PTB[:, 0, c0:c0 + n], in_=o, func=AF.Identity, bias=vcol(R_BIN + 32 + h)), r=[k, "VEC"], w=["T3"])
            if PSUB < 3:
                continue
            if int(os.environ.get("KP3", "9")) >= 1:
                S.op("act", lambda e, h=h: e.activation(out=PT[:, 1, :], in_=PT[:, 0, :], func=AF.Ln, scale=DER[:, 8 + h:9 + h], bias=DER[:, h:h + 1]), r=["T0", "DER"], w=["T1"])
            S.op("act", lambda e, h=h: e.activation(out=PT[:, 0, :], in_=PT[:, 0, :], func=AF.Identity, scale=DER[:, 16 + h:17 + h], bias=DER[:, 8 + h:9 + h]), r=["T0", "DER"], w=["T0"])
            if int(os.environ.get("KP3", "9")) < 2:
                continue
            PX = RC[:, 4096:5120]
            if USE_SCAN:
                S.op("dve", lambda e: e.tensor_tensor_scan(out=PT[:, 2, :], data0=CON[:, C_PRM:C_PRM + 1].to_broadcast([128, 1024]), data1=PT[:, 1, :], initial=0.0, op0=ALU.mult, op1=ALU.add), r=["T1", "CON"], w=["T2"])
            else:
                cumsum_steps(PT[:, 1, :], PX, PT[:, 2, :], "T7", "T2", "T1", 1024, 1)
            if int(os.environ.get("KP3", "9")) < 3:
                continue
            S.op("act", lambda e: e.activation(out=PT[:, 1, :], in_=PT[:, 2, :], func=AF.Exp, scale=-1.0, bias=PT[:, 2, 1023:1024]), r=["T2"], w=["T1"])
            S.op("dve", lambda e: e.tensor_tensor(out=PTB[:, 1, :], in0=PT[:, 0, :], in1=PT[:, 1, :], op=ALU.mult), r=["T0", "T1"], w=["T4"])
            if PSUB < 4:
                continue
            bk, k = transposes([(PTB[:, 0, ti * 128:(ti + 1) * 128], 128, 128, ti * 128) for ti in range(8)], IDB, BF16, ["T3", "IDB"])
            copy_op(ev_eng(), PVB[:, 0:8, :], bk[:, 0:1024].rearrange("p (a b) -> p a b", a=8), r=[k], w=["T5"])
            bk, k = transposes([(PTB[:, 1, ti * 128:(ti + 1) * 128], 128, 128, ti * 128) for ti in range(8)], IDB, BF16, ["T4", "IDB"])
            copy_op(ev_eng(), PVB[:, 8:16, :], bk[:, 0:1024].rearrange("p (a b) -> p a b", a=8), r=[k], w=["T6"])
            rg = ring_next()
            su = PS[:, rg, 0:128]
            mm_group(su, [(PVB[:, 8 + ti, :], PVB[:, ti, :]) for ti in range(8)], r=["T5", "T6"], w=[f"ps{rg}"])
            S.op("dve", lambda e, h=h, su=su: e.tensor_scalar(out=S32[:, h, :], in0=su, scalar1=FLG[:, 0:1], scalar2=None, op0=ALU.mult), r=[f"ps{rg}", "FLG"], w=[f"S32_{h}"])
            S.op("act", lambda e, h=h: e.activation(out=SBF[:, h, :], in_=S32[:, h, :], func=AF.Identity), r=[f"S32_{h}"], w=[f"SBF_{h}"])
        allk = [f"X{k}" for k in range(16)] + [f"T{k}" for k in range(8)] + [f"CP{k}" for k in range(8)] + ["GS_0"]
        S.op("act", lambda e: e.activation(out=DBL[:, 15:16], in_=DER[:, 56:57], func=AF.Identity), r=["EPSK"], w=allk + ["DBLx"])

    def emit_pass(p):
        XK = [f"B{k}" for k in range(16)]
        _emit_pass_body(p, XK)

    def _emit_pass_body(p, XK):
        for ti in range(4):
            load_xT(xm[p * TM + ti * 128: p * TM + (ti + 1) * 128, :], 128, XTB, XT32, ti * 128, ["B", "X"])
        load_xT(xs[p * 32:(p + 1) * 32, :], 32, XTB, XT32, TM, ["B", "X"])
        if p == 0:
            load_xT(xh[:, :], 32, XHT, None, 0, ["XH"])
        rhsx = lambda kc, c0, n: XTB[:, kc, c0:c0 + n]

        KPASS = int(os.environ.get("KPASS", "9"))
        KCH = int(os.environ.get("KCH", "9"))
        if KPASS < 1:
            return
        SG = TS[:, 6, :]
        def conv_chunk(c):
            ub = c % 2
            ue = UE[:, ub, :]
            uek = f"UE{ub}"
            wb, kb = wnext(16, U_IN + 8 + c)
            for (c0, n) in TT:
                o, k = proj(wb, kb, 16, rhsx, XK, c0, n)
                S.op("act", lambda e, o=o, c0=c0, n=n, c=c: e.activation(out=SG[:, c0:c0 + n], in_=o, func=AF.Sigmoid, bias=vcol(R_BIN + 8 + c)), r=[k, "VEC"], w=["T6"])
            if p == 0:
                o, k = proj(wb, kb, 16, lambda kc, c0, n: XHT[:, kc, 0:32], [f"XH{q}" for q in range(16)], 0, 32)
                S.op("act", lambda e, o=o, c=c: e.activation(out=UHS[:, 0, :], in_=o, func=AF.Sigmoid, bias=vcol(R_BIN + 8 + c)), r=[k, "VEC"], w=["UHS0"])
            yield
            wa, ka = wnext(16, U_IN + c)
            if p == 1:
                S.op("act", lambda e, ue=ue, c=c: e.activation(out=ue[:, 0:30], in_=UH[:, p, c, :], func=AF.Identity), r=[f"UH{p}_{c}"], w=[uek])
            S.op("act", lambda e, ue=ue, c=c: e.activation(out=ue[:, SMP0:SMP0 + 92].rearrange("p (s j) -> p s j", s=2)[:, :, 0:30], in_=CACHE[:, c, 2 * p:2 * p + 2, :], func=AF.Identity), r=["CACHE"], w=[uek])
            for (c0, n) in TT:
                o, k = proj(wa, ka, 16, rhsx, XK, c0, n)
                nm = min(c0 + n, TM) - c0
                if nm > 0:
                    S.op("dve", lambda e, o=o, c0=c0, nm=nm, ue=ue, c=c: e.scalar_tensor_tensor(out=ue[:, 30 + c0:30 + c0 + nm], in0=o[:, 0:nm], scalar=vcol(R_BIN + c), in1=SG[:, c0:c0 + nm], op0=ALU.add, op1=ALU.mult), r=[k, "T6", "VEC"], w=[uek])
                if c0 + n > TM:
                    so = max(TM - c0, 0)
                    S.op("dve", lambda e, o=o, so=so, ue=ue, c=c: e.scalar_tensor_tensor(
                        out=ue[:, SMP0:SMP0 + 92].rearrange("p (s j) -> p s j", s=2)[:, :, 30:46],
                        in0=o[:, so:so + 32].rearrange("p (s j) -> p s j", s=2), scalar=vcol(R_BIN + c),
                        in1=SG[:, TM:TM + 32].rearrange("p (s j) -> p s j", s=2), op0=ALU.add, op1=ALU.mult), r=[k, "T6", "VEC"], w=[uek])
            if p == 0:
                o, k = proj(wa, ka, 16, lambda kc, c0, n: XHT[:, kc, 0:32], [f"XH{q}" for q in range(16)], 0, 32)
                S.op("dve", lambda e, o=o, c=c: e.scalar_tensor_tensor(out=UHS[:, 1, :], in0=o, scalar=vcol(R_BIN + c), in1=UHS[:, 0, :], op0=ALU.add, op1=ALU.mult), r=[k, "UHS0", "VEC"], w=["UHS1"])
                S.op("act", lambda e, c=c: e.activation(out=UH[:, 0, c, :], in_=UHS[:, 1, 2:32], func=AF.Identity, scale=FLG[:, 0:1]), r=["UHS1", "FLG"], w=[f"UH0_{c}"])
                S.op("act", lambda e, ue=ue, c=c: e.activation(out=ue[:, 0:30], in_=UH[:, 0, c, :], func=AF.Identity), r=[f"UH0_{c}"], w=[uek])
            S.op("act", lambda e, ue=ue, ub=ub: e.activation(out=UEB[:, ub, 0:634], in_=ue[:, 0:634], func=AF.Identity), r=[uek], w=[f"UEB{ub}"])

            def conv_part(c=c, ub=ub):
                wtap = VEC[:, R_WDW:R_WDW + 248].rearrange("p (j c) -> p j c", c=8)[:, :, c]
                S.op("pool", lambda e, wtap=wtap: e.tensor_tensor(out=DG[:, :, :], in0=IDB[:, :].unsqueeze(1).to_broadcast([128, 31, 128]),
                                                                   in1=wtap.unsqueeze(2).to_broadcast([128, 31, 128]), op=ALU.mult), r=["IDB", "VEC"], w=["DG"])
                for (c0, n) in TT:
                    rg = ring_next()
                    nm = min(c0 + n, TM) - c0

                    def fn(e, rg=rg, c0=c0, n=n, nm=nm, ub=ub):
                        ins = None
                        for j in range(31):
                            ins = e.matmul(PS[:, rg, 0:nm], lhsT=DG[:, j, :], rhs=UEB[:, ub, c0 + j:c0 + j + nm], start=(j == 0), stop=(j == 30))
                        if nm < n:
                            for s_ in range(2):
                                for j in range(31):
                                    ins = e.matmul(PS[:, rg, nm + 16 * s_:nm + 16 * s_ + 16], lhsT=DG[:, j, :],
                                                   rhs=UEB[:, ub, SMP0 + 46 * s_ + j:SMP0 + 46 * s_ + j + 16], start=(j == 0), stop=(j == 30))
                        return ins
                    S.op("pe", fn, r=["DG", f"UEB{ub}"], w=[f"ps{rg}"])
                    S.op("act", lambda e, rg=rg, c0=c0, n=n, c=c: e.activation(out=CP[:, c, c0:c0 + n], in_=PS[:, rg, 0:n], func=AF.Identity, bias=vcol(R_BDW + c)), r=[f"ps{rg}", "VEC"], w=[f"CP{c}"])
            yield
            conv_part()
            yield
            S.op("act", lambda e, ue=ue, c=c: e.activation(out=UH[:, 1 - p, c, :], in_=ue[:, TM:TM + 30], func=AF.Identity), r=[uek], w=[f"UH{1 - p}_{c}"])
            outs = [(ue[:, SMP0 + 46 * s + 16: SMP0 + 46 * s + 48], cs_o[2 * p + s, :, c * 128:(c + 1) * 128], [uek]) for s in range(2)]
            if p == 1:
                outs.append((UH[:, :, :, :].rearrange("p a b c -> p (a b c)")[:, c * 30:c * 30 + 32], cm[:, c * 128:(c + 1) * 128], [f"UH0_{c}"]))
            rks = ["CON"]
            for (_, _, rk) in outs:
                rks += rk
            bk, k = transposes([(src, 128, 32, i * 128) for i, (src, _, _) in enumerate(outs)], ID32, F32, rks)
            cb = tp_ctr[0] % 2
            tp_ctr[0] += 1
            no = len(outs)
            copy_op(ev_eng(), CST[0:32, cb, 0:no * 128], bk[0:32, 0:no * 128], r=[k], w=[f"CST{cb}"])
            for i, (_, dst, _) in enumerate(outs):
                S.op("sp", lambda e, cb=cb, dst=dst, i=i: e.dma_start(out=dst, in_=CST[0:30, cb, i * 128:(i + 1) * 128]), r=[f"CST{cb}"], dma=f"cst{cb}")
            out_dmas.append(f"cst{cb}")
        def conv_ln():
            layer_norm(lambda c: CP[:, c, :], [f"CP{c}" for c in range(8)], 8, ONC, "ONC",
                       lambda c, t, tk: S.op("act", lambda e: e.activation(out=CAT[:, c, :], in_=t, func=AF.Silu, scale=vcol(R_CNG + c), bias=vcol(R_CNB + c)), r=[tk, "VEC"], w=[f"CAT{c}"]))

        QS, SGK, LF, LC, _, O32 = (TS[:, i, :] for i in range(6))
        XINF = XIN[:, :, :].rearrange("p a b -> p (a b)")
        PB = [
            dict(BTQ=BT[:, 0:2, :], VB=VB, KEB=KEB, KES=KES, DBL=DBL, GS=TS[:, 4, :]),
            dict(BTQ=XINF[:, 544:1088].bitcast(BF16).rearrange("p (a b) -> p a b", a=2),
                 VB=XINF[:, 1088:1664].bitcast(BF16).rearrange("p (a b) -> p a b", a=9),
                 KEB=XINF[:, 1664:2240].bitcast(BF16).rearrange("p (a b) -> p a b", a=9),
                 KES=XINF[:, 2240:2368].bitcast(BF16).rearrange("p (a b) -> p a b", a=2),
                 DBL=XINF[:, 2368:2384], GS=XINF[:, 0:544]),
        ]
        AK = (["BT0_1", "BT1_1", "DBL_1", "GS_1", "KES0_1", "KES1_1"] + [f"VB{j}_1" for j in range(9)] + [f"KEB{j}_1" for j in range(8)])

        def fence(keys):
            S.op("act", lambda e: e.activation(out=DBL[:, 15:16], in_=DER[:, 56:57], func=AF.Identity), r=["EPSK"], w=list(keys) + ["DBLx"])
        fence(["XIN0", "XIN1"] + AK)

        def phaseA(h):
            par = h % 2
            b = par
            B_ = PB[par]
            BTQ, VBp, KEBp, KESp, DBLp, GSp = B_["BTQ"], B_["VB"], B_["KEB"], B_["KES"], B_["DBL"], B_["GS"]
            kx = f"_{par}"
            for s_ in range(2):
                S.op("sp", lambda e, s_=s_: e.dma_start(out=SS0[:, b, s_, :], in_=sh[2 * p + s_, h, :, :]), w=[f"SS0_{b}"], dma=f"ss0_{b}")
            S.op("act", lambda e: e.activation(out=SS0B[:, b, :, :], in_=SS0[:, b, :, :], func=AF.Identity), r=[f"SS0_{b}"], w=[f"SS0B_{b}"])
            for (dst, key, func, brow, silu) in ((QS, "T0", AF.Sigmoid, 16, True), (SGK, "T1", AF.Sigmoid, 24, False),
                                                 (BT[:, 3, :], "BT3", AF.Identity, 32, False), (GSp, "GS" + kx, AF.Sigmoid, 40, True)):
                wv, kw = wnext(16, U_IN + brow + h)
                for (c0, n) in TT:
                    rg = ring_next()
                    o = PS[:, rg, 0:n]
                    k = f"ps{rg}"
                    for half in range(2):
                        def fn(e, o=o, wv=wv, c0=c0, n=n, half=half):
                            ins = None
                            for kc in range(8 * half, 8 * half + 8):
                                ins = e.matmul(o, lhsT=wv[:, kc, :], rhs=XTB[:, kc, c0:c0 + n], start=(kc == 0), stop=(kc == 15))
                            return ins
                        S.op("pe", fn, r=[kw] + XK, w=[k])
                        if half == 0:
                            yield
                    S.op("act", lambda e, o=o, c0=c0, n=n, dst=dst, func=func, brow=brow: e.activation(out=dst[:, c0:c0 + n], in_=o, func=func, bias=vcol(R_BIN + brow + h)), r=[k, "VEC"], w=[key])
                    if silu:
                        S.op("dve", lambda e, o=o, c0=c0, n=n, dst=dst, brow=brow: e.scalar_tensor_tensor(out=dst[:, c0:c0 + n], in0=o, scalar=vcol(R_BIN + brow + h), in1=dst[:, c0:c0 + n], op0=ALU.add, op1=ALU.mult), r=[k, key, "VEC"], w=[key])
                    yield
            yield "P"
            S.op("act", lambda e: e.activation(out=LF, in_=SGK, func=AF.Ln, scale=DER[:, 8 + h:9 + h], bias=DER[:, h:h + 1]), r=["T1", "DER"], w=["T2"])
            S.op("act", lambda e: e.activation(out=SGK, in_=SGK, func=AF.Identity, scale=DER[:, 16 + h:17 + h], bias=DER[:, 8 + h:9 + h]), r=["T1", "DER"], w=["T1"])
            S.op("dve", lambda e: e.tensor_tensor_scan(out=LC, data0=CON[:, C_RMASK:C_RMASK + T], data1=LF, initial=0.0, op0=ALU.mult, op1=ALU.add), r=["T2", "CON"], w=["T3"])
            yield
            S.op("act", lambda e: e.activation(out=DBLp[:, 0:8], in_=LC[:, 0:TM].rearrange("p (j t) -> p j t", t=64)[:, :, 63], func=AF.Exp), r=["T3"], w=["DBL" + kx])
            S.op("act", lambda e: e.activation(out=DBLp[:, 8:10], in_=LC[:, TM:T].rearrange("p (j t) -> p j t", t=16)[:, :, 15], func=AF.Exp), r=["T3"], w=["DBL" + kx])
            S.op("act", lambda e: e.activation(out=LF, in_=LC, func=AF.Exp, scale=-1.0), r=["T3"], w=["T2"])
            S.op("act", lambda e: e.activation(out=LC, in_=LC, func=AF.Exp), r=["T3"], w=["T3"])
            yield
            S.op("dve", lambda e: e.tensor_tensor(out=BTQ[:, 0, :], in0=QS, in1=LC, op=ALU.mult), r=["T0", "T3"], w=["BT0" + kx])
            S.op("dve", lambda e: e.tensor_tensor(out=LF, in0=SGK, in1=LF, op=ALU.mult), r=["T1", "T2"], w=["T2"])
            S.op("act", lambda e: e.activation(out=BTQ[:, 1, :], in_=LF, func=AF.Identity), r=["T2"], w=["BT1" + kx])
            yield
            S.op("dve", lambda e: e.tensor_tensor(out=BT[:, 2, 0:TM].rearrange("p (j t) -> p j t", t=64), in0=LF[:, 0:TM].rearrange("p (j t) -> p j t", t=64),
                                                  in1=DBLp[:, 0:8].unsqueeze(2).to_broadcast([128, 8, 64]), op=ALU.mult), r=["T2", "DBL" + kx], w=["BT2"])
            S.op("dve", lambda e: e.tensor_tensor(out=BT[:, 2, TM:T].rearrange("p (j t) -> p j t", t=16), in0=LF[:, TM:T].rearrange("p (j t) -> p j t", t=16),
                                                  in1=DBLp[:, 8:10].unsqueeze(2).to_broadcast([128, 2, 16]), op=ALU.mult), r=["T2", "DBL" + kx], w=["BT2"])
            yield
            bk, k = transposes([(BT[:, 3, 64 * j:64 * j + 64], 128, 64, j * 128) for j in range(8)], IDB, BF16, ["BT3", "IDB"])
            copy_op(ev_eng(), VBp[0:64, 0:8, :], bk[0:64, 0:1024].rearrange("p (a b) -> p a b", a=8), r=[k], w=[f"VB{j}{kx}" for j in range(8)])
            yield
            bk, k = transposes([(BT[:, 2, 64 * j:64 * j + 64], 128, 64, j * 128) for j in range(8)], IDB, BF16, ["BT2", "IDB"])
            copy_op(ev_eng(), KEBp[0:64, 0:8, :], bk[0:64, 0:1024].rearrange("p (a b) -> p a b", a=8), r=[k], w=[f"KEB{j}{kx}" for j in range(8)])
            yield
            bk, k = transposes([(BT[:, 3, TM:T], 128, 32, 0), (BT[:, 2, TM:T], 128, 32, 128)], IDB, BF16, ["BT3", "BT2", "IDB"])
            copy_op("dve", VBp[0:32, 8, :], bk[0:32, 0:128], r=[k], w=["VB8" + kx])
            for s_ in range(2):
                S.op("dve", lambda e, bk=bk, s_=s_: e.tensor_scalar(out=KESp[0:32, s_, :], in0=bk[0:32, 128:256], scalar1=CON[0:32, C_SEQM + s_:C_SEQM + s_ + 1], scalar2=None, op0=ALU.mult), r=[k, "CON"], w=[f"KES{s_}{kx}"])
            yield

        def phaseB(h):
            par = h % 2
            b = par
            B_ = PB[par]
            BTQ, VBp, KEBp, KESp, DBLp, GSp = B_["BTQ"], B_["VB"], B_["KEB"], B_["KES"], B_["DBL"], B_["GS"]
            kx = f"_{par}"
            for j in range(9):
                sl = j % 2
                if j < 8:
                    c0, n = 64 * j, 64
                    mask = CON[0:64, C_CAUS:C_CAUS + 64]
                else:
                    c0, n = TM, 32
                    mask = CON[0:32, C_SMASK:C_SMASK + 32]
                rg = ring_next()
                sc = PS[0:n, rg, 0:n]
                mm_group(sc, [(BTQ[:, 1, c0:c0 + n], BTQ[:, 0, c0:c0 + n])], r=["BT0" + kx, "BT1" + kx], w=[f"ps{rg}"])
                scm = SCM[0:n, sl, 0:n]
                S.op("dve", lambda e, sc=sc, scm=scm, mask=mask: e.tensor_tensor(out=scm, in0=sc, in1=mask, op=ALU.mult), r=[f"ps{rg}", "CON"], w=[f"SCM{sl}"])
                yield
                rg = ring_next()
                ops_ = PS[:, rg, 0:n]
                pk = f"ps{rg}"
                rg2 = ring_next()
                pk2 = f"ps{rg2}"
                if j < 8:
                    su = PS[:, rg2, 0:128]
                    mm_group(su, [(KEBp[0:64, j, :], VBp[0:64, j, :])], r=[f"VB{j}{kx}", f"KEB{j}{kx}"], w=[pk2])
                    s32src, k32src = (S32[:, h, :], f"S32_{h}") if j == 0 else (SPP[:, (j - 1) % 2, :], f"SPP{(j - 1) % 2}")
                    s32dst, k32dst = (S32[:, h, :], f"S32_{h}") if j == 7 else (SPP[:, j % 2, :], f"SPP{j % 2}")
                    sbfsrc, kbfsrc = (SBF[:, h, :], f"SBF_{h}") if j == 0 else (SBP[:, (j - 1) % 2, :], f"SBP{(j - 1) % 2}")
                    sbfdst, kbfdst = (SBF[:, h, :], f"SBF_{h}") if j == 7 else (SBP[:, j % 2, :], f"SBP{j % 2}")
                    mm_group(ops_, [(VBp[0:n, j, :], scm), (sbfsrc, BTQ[:, 0, c0:c0 + n])], r=[f"VB{j}{kx}", f"SCM{sl}", kbfsrc, "BT0" + kx], w=[pk])
                    S.op("act", lambda e, ops_=ops_, c0=c0, n=n: e.activation(out=O32[:, c0:c0 + n], in_=ops_, func=AF.Identity), r=[pk], w=["T5"])
                    S.op("dve", lambda e, su=su, j=j, s32src=s32src, s32dst=s32dst: e.scalar_tensor_tensor(out=s32dst, in0=s32src, scalar=DBLp[:, j:j + 1], in1=su, op0=ALU.mult, op1=ALU.add), r=[pk2, "DBL" + kx, k32src], w=[k32dst])
                    S.op("act", lambda e, s32dst=s32dst, sbfdst=sbfdst: e.activation(out=sbfdst, in_=s32dst, func=AF.Identity), r=[k32dst], w=[kbfdst])
                else:
                    def fn(e, ops_=ops_, scm=scm):
                        e.matmul(ops_, lhsT=VBp[0:32, 8, :], rhs=scm, start=True, stop=False)
                        e.matmul(ops_[:, 0:16], lhsT=SS0B[:, b, 0, :], rhs=BTQ[:, 0, TM:TM + 16], start=False, stop=False)
                        return e.matmul(ops_[:, 16:32], lhsT=SS0B[:, b, 1, :], rhs=BTQ[:, 0, TM + 16:TM + 32], start=False, stop=True)
                    S.op("pe", fn, r=["VB8" + kx, f"SCM{sl}", f"SS0B_{b}", "BT0" + kx], w=[pk])

                    def fn2(e, rg2=rg2):
                        e.matmul(PS[:, rg2, 0:128], lhsT=KESp[0:32, 0, :], rhs=VBp[0:32, 8, :], start=True, stop=True)
                        return e.matmul(PS[:, rg2, 128:256], lhsT=KESp[0:32, 1, :], rhs=VBp[0:32, 8, :], start=True, stop=True)
                    S.op("pe", fn2, r=["VB8" + kx, "KES0" + kx, "KES1" + kx], w=[pk2])
                    S.op("act", lambda e, ops_=ops_, c0=c0, n=n: e.activation(out=O32[:, c0:c0 + n], in_=ops_, func=AF.Identity), r=[pk], w=["T5"])
                    for s_ in range(2):
                        S.op("dve", lambda e, rg2=rg2, s_=s_: e.scalar_tensor_tensor(out=SS1[:, b, s_, :], in0=SS0[:, b, s_, :], scalar=DBLp[:, 8 + s_:9 + s_], in1=PS[:, rg2, 128 * s_:128 + 128 * s_], op0=ALU.mult, op1=ALU.add), r=[pk2, "DBL" + kx, f"SS0_{b}"], w=[f"SS1_{b}"])
                    for s_ in range(2):
                        S.op("sp", lambda e, s_=s_: e.dma_start(out=hs_o[2 * p + s_, h, :, :], in_=SS1[:, b, s_, :]), r=[f"SS1_{b}"], dma=f"ss1_{b}")
                    out_dmas.append(f"ss1_{b}")
                yield
            if p == 1:
                S.op("sp", lambda e: e.dma_start(out=hm[h, :, :], in_=S32[:, h, :]), r=[f"S32_{h}"], dma="hmo")
                out_dmas.append("hmo")
            S.op("act", lambda e: e.activation(out=SCR[:, 0, :], in_=O32, func=AF.Square), r=["T5"], w=["SCR0"])
            for (c0, n) in TT:
                rg = ring_next()
                ssp = PS[:, rg, 0:n]
                mm_group(ssp, [(ONH[:, :], SCR[:, 0, c0:c0 + n])], r=["ONH", "SCR0"], w=[f"ps{rg}"])
                S.op("act", lambda e, ssp=ssp, n=n: e.activation(out=LNS[:, 2, 0:n], in_=ssp, func=AF.Ln, bias=DER[:, 56:57]), r=[f"ps{rg}", "EPSK"], w=["LNS2"])
                S.op("act", lambda e, n=n: e.activation(out=LNS[:, 2, 0:n], in_=LNS[:, 2, 0:n], func=AF.Exp, scale=-0.5), r=["LNS2"], w=["LNS2"])
                S.op("dve", lambda e, c0=c0, n=n: e.scalar_tensor_tensor(out=O32[:, c0:c0 + n], in0=O32[:, c0:c0 + n], scalar=vcol(R_HNG), in1=LNS[:, 2, 0:n], op0=ALU.mult, op1=ALU.mult), r=["T5", "LNS2", "VEC"], w=["T5"])
                S.op("dve", lambda e, c0=c0, n=n: e.tensor_tensor(out=CAT[:, 8 + h, c0:c0 + n], in0=O32[:, c0:c0 + n], in1=GSp[:, c0:c0 + n], op=ALU.mult), r=["T5", "GS" + kx], w=[f"CAT{8 + h}"])
                yield

        def merge(gens):
            gens = [g for g in gens if g is not None]
            while gens:
                for g in list(gens):
                    try:
                        next(g)
                    except StopIteration:
                        gens.remove(g)
        wlook[0] = NSLOT - 2
        def run_until(g, tok):
            for v in g:
                if v == tok:
                    return
        ga = phaseA(0)
        run_until(ga, "P")
        merge([ga, conv_chunk(0)])
        for h in range(NH):
            if h + 1 < NH:
                ga = phaseA(h + 1)
                run_until(ga, "P")
                merge([ga, phaseB(h), conv_chunk(h + 1)])
            else:
                merge([phaseB(h)])
        wlook[0] = NSLOT
        conv_ln()
        fence(["XIN0", "XIN1"] + AK)

        if KPASS < 4:
            return
        CK = [f"CAT{k}" for k in range(16)]
        for dc in range(16):
            wv, kw = wnext(16, U_OUT + dc)
            for (c0, n) in TT:
                o, k = proj(wv, kw, 16, lambda kc, c0, n: CAT[:, kc, c0:c0 + n], CK, c0, n)
                S.op("dve", lambda e, o=o, c0=c0, n=n, dc=dc: e.scalar_tensor_tensor(out=XT32[:, dc, c0:c0 + n], in0=XT32[:, dc, c0:c0 + n], scalar=ALPHA, in1=o, op0=ALU.mult, op1=ALU.add), r=[k, f"X{dc}"], w=[f"X{dc}"])

        def ln1_out(c, t, tk):
            S.op("act", lambda e: e.activation(out=XTB[:, c, :], in_=t, func=AF.Identity, scale=vcol(R_L1G + c), bias=vcol(R_L1B + c)), r=[tk, "VEC"], w=[f"B{c}"])
            S.op("act", lambda e: e.activation(out=XT32[:, c, :], in_=t, func=AF.Identity, scale=DER[:, 24 + c:25 + c], bias=DER[:, 40 + c:41 + c]), r=[tk, "DER"], w=[f"X{c}"])
        layer_norm(lambda c: XT32[:, c, :], [f"X{c}" for c in range(16)], 16, OND, "OND", ln1_out)

        if KPASS < 5:
            return
        for g in range(11):
            ab = g % 2
            for f in range(4):
                wg, kg_ = wnext(16, U_GATE + 4 * g + f)
                gps = []
                for (c0, n) in TT:
                    o, k = proj(wg, kg_, 16, rhsx, XK, c0, n)
                    sgs = TS[:, len(gps), 0:n]
                    tk = f"T{len(gps)}"
                    S.op("act", lambda e, o=o, sgs=sgs: e.activation(out=sgs, in_=o, func=AF.Silu), r=[k], w=[tk])
                    gps.append((sgs, tk))
                wu, ku = wnext(16, U_UP + 4 * g + f)
                for i, (c0, n) in enumerate(TT):
                    o, k = proj(wu, ku, 16, rhsx, XK, c0, n)
                    sgs, tk = gps[i]
                    ak = f"CP{(ab * 4 + f) // 2}"
                    S.op("dve", lambda e, o=o, sgs=sgs, c0=c0, n=n, ab=ab, f=f: e.tensor_tensor(out=ACTB[:, ab, f, c0:c0 + n], in0=sgs, in1=o, op=ALU.mult), r=[k, tk], w=[ak])
            aks = [f"CP{ab * 2}", f"CP{ab * 2 + 1}"]
            for dc in range(16):
                if dc % 4 == 0:
                    wv, kw = wnext(4, U_DOWN + 4 * g + dc // 4)
                sub = ((dc % 4) * 128, (dc % 4) * 128 + 128)
                for (c0, n) in TT:
                    o, k = proj(wv, kw, 4, lambda kc, c0, n, ab=ab: ACTB[:, ab, kc, c0:c0 + n], aks, c0, n, wsub=sub)
                    S.op("dve", lambda e, o=o, c0=c0, n=n, dc=dc: e.tensor_tensor(out=XT32[:, dc, c0:c0 + n], in0=XT32[:, dc, c0:c0 + n], in1=o, op=ALU.add), r=[k, f"X{dc}"], w=[f"X{dc}"])

        if KPASS < 6:
            return
        def ln2_out(c, t, tk):
            S.op("act", lambda e: e.activation(out=XT32[:, c, :], in_=t, func=AF.Identity, scale=vcol(R_L2G + c), bias=vcol(R_L2B + c)), r=[tk, "VEC"], w=[f"X{c}"])
        layer_norm(lambda c: XT32[:, c, :], [f"X{c}" for c in range(16)], 16, OND, "OND", ln2_out)

        for ti in range(5):
            c0, n = (ti * 128, 128) if ti < 4 else (TM, 32)
            b = xbuf[0]
            xbuf[0] ^= 1
            for g in range(4):
                rg = ring_next()

                def fn(e, g=g, rg=rg, c0=c0, n=n):
                    ins = None
                    for q in range(4):
                        ins = e.transpose(PS[0:n, rg, q * 128:(q + 1) * 128], XT32[:, 4 * g + q, c0:c0 + n], ID32)
                    return ins
                S.op("pe", fn, r=[f"X{4 * g + q}" for q in range(4)] + ["CON"], w=[f"ps{rg}"])
                copy_op(ev_eng(), XIN[0:n, b, g * 512:(g + 1) * 512], PS[0:n, rg, :], r=[f"ps{rg}"], w=[f"XIN{b}"])
            dst = ym[p * TM + c0: p * TM + c0 + n, :] if ti < 4 else ys[p * 32:(p + 1) * 32, :]
            S.op("sp", lambda e, b=b, n=n, dst=dst: e.dma_start(out=dst, in_=XIN[0:n, b, :]), r=[f"XIN{b}"], dma=f"xin{b}")
            out_dmas.append(f"xin{b}")

    def layer_norm(src, skeys, nch, ones, okey, out_fn):
        banks = [ring_next() for _ in range(4)]
        bkeys = [f"ps{b_}" for b_ in banks]
        for c in range(nch):
            sl = c % 2
            cb = SCR[:, 2 * sl, :]
            sq = SCR[:, 2 * sl + 1, :]
            S.op("act", lambda e, c=c, cb=cb: e.activation(out=cb, in_=src(c), func=AF.Identity), r=[skeys[c]], w=[f"SCR{2 * sl}"])
            S.op("act", lambda e, c=c, sq=sq: e.activation(out=sq, in_=src(c), func=AF.Square), r=[skeys[c]], w=[f"SCR{2 * sl + 1}"])

            def fn(e, c=c, cb=cb, sq=sq):
                ins = None
                for ti, (c0, n) in enumerate(TT):
                    e.matmul(PS[:, banks[2 * ti], 0:n], lhsT=ones[:, :], rhs=cb[:, c0:c0 + n], start=(c == 0), stop=(c == nch - 1))
                    ins = e.matmul(PS[:, banks[2 * ti + 1], 0:n], lhsT=ones[:, :], rhs=sq[:, c0:c0 + n], start=(c == 0), stop=(c == nch - 1))
                return ins
            S.op("pe", fn, r=[okey, f"SCR{2 * sl}", f"SCR{2 * sl + 1}"], w=bkeys)
        for ti, (c0, n) in enumerate(TT):
            MEAN, VAR, RSTD = LNS[:, 0, c0:c0 + n], LNS[:, 1, c0:c0 + n], LNS[:, 2, c0:c0 + n]
            mps, qps = PS[:, banks[2 * ti], 0:n], PS[:, banks[2 * ti + 1], 0:n]
            S.op("act", lambda e, mps=mps, MEAN=MEAN: e.activation(out=MEAN, in_=mps, func=AF.Identity), r=[bkeys[2 * ti]], w=["LNS0"])
            S.op("dve", lambda e, MEAN=MEAN, VAR=VAR: e.tensor_tensor(out=VAR, in0=MEAN, in1=MEAN, op=ALU.mult), r=["LNS0"], w=["LNS1"])
            S.op("dve", lambda e, qps=qps, VAR=VAR: e.tensor_tensor(out=VAR, in0=qps, in1=VAR, op=ALU.subtract), r=[bkeys[2 * ti + 1], "LNS1"], w=["LNS1"])
            S.op("act", lambda e, VAR=VAR, RSTD=RSTD: e.activation(out=RSTD, in_=VAR, func=AF.Ln, bias=DER[:, 56:57]), r=["LNS1", "EPSK"], w=["LNS2"])
            S.op("act", lambda e, RSTD=RSTD: e.activation(out=RSTD, in_=RSTD, func=AF.Exp, scale=-0.5), r=["LNS2"], w=["LNS2"])
        for c in range(nch):
            t = TS[:, 6 + (c % 2), :]
            tk = f"T{6 + (c % 2)}"
            S.op("dve", lambda e, c=c, t=t: e.tensor_tensor(out=t, in0=src(c), in1=LNS[:, 0, :], op=ALU.subtract), r=[skeys[c], "LNS0"], w=[tk])
            S.op("dve", lambda e, t=t: e.tensor_tensor(out=t, in0=t, in1=LNS[:, 2, :], op=ALU.mult), r=[tk, "LNS2"], w=[tk])
            out_fn(c, t, tk)

    tp_ctr = [0]
    out_dmas = []
    import os
    STAGE = int(os.environ.get("KSTAGE", "9"))
    if STAGE >= 1:
        emit_prefix()
    if STAGE >= 2:
        emit_pass(0)
    if STAGE >= 9:
        emit_pass(1)
    fin = S.op("sp", None, r=[], w=[])
    fin.dmadeps = {st: 16 * S.dma_cnt[st] for st in set(out_dmas)}
    S.finalize()

    prog = {e: es.enter_context(nc.semaphore(f"prog_{e}")) for e in S.ENG}
    dmasems = {k: es.enter_context(nc.semaphore(f"dma_{k}")) for k in S.dma_cnt}
    with nc.Block() as block:
        @block.tensor
        def _(eng):
            S.replay("pe", eng, prog, dmasems)

        @block.scalar
        def _(eng):
            S.replay("act", eng, prog, dmasems)

        @block.vector
        def _(eng):
            S.replay("dve", eng, prog, dmasems)

        @block.gpsimd
        def _(eng):
            S.replay("pool", eng, prog, dmasems)

        @block.sync
        def _(eng):
            S.replay("sp", eng, prog, dmasems)
    es.close()
    return nc, wrec


def _consts():
    c = np.zeros((128, NCONST), np.float32)
    c[:, C_ID:C_ID + 128] = np.eye(128, dtype=np.float32)
    s = np.arange(64)
    c[0:64, C_CAUS:C_CAUS + 64] = (s[:, None] <= s[None, :]).astype(np.float32)
    s = np.arange(32)
    c[0:32, C_SMASK:C_SMASK + 32] = ((s[:, None] <= s[None, :]) & ((s[:, None] // 16) == (s[None, :] // 16))).astype(np.float32)
    c[0:16, C_SEQM] = 1.0
    c[16:32, C_SEQM + 1] = 1.0
    rm = np.ones(T, np.float32)
    rm[0:TM:64] = 0.0
    rm[TM:T:16] = 0.0
    c[:, C_RMASK:C_RMASK + T] = rm[None, :]
    c[:, C_PRM:C_PRM + 2] = 1.0
    return c


def _vecs(b_in, w_dw, b_dw, cng, cnb, lbs, hng, l1g, l1b, l2g, l2b):
    v = np.zeros((NVEC, 128), np.float32)
    v[R_BIN:R_BIN + 48] = b_in.reshape(48, 128)
    v[R_WDW:R_WDW + 248] = w_dw.reshape(31 * 8, 128)
    v[R_BDW:R_BDW + 8] = b_dw.reshape(8, 128)
    v[R_CNG:R_CNG + 8] = cng.reshape(8, 128)
    v[R_CNB:R_CNB + 8] = cnb.reshape(8, 128)
    v[R_LB0:R_LB0 + 8] = lbs[0].reshape(8, 128)
    v[R_LB1:R_LB1 + 8] = lbs[1].reshape(8, 128)
    v[R_HNG] = hng.reshape(128)
    v[R_L1G:R_L1G + 16] = l1g.reshape(16, 128)
    v[R_L1B:R_L1B + 16] = l1b.reshape(16, 128)
    v[R_L2G:R_L2G + 16] = l2g.reshape(16, 128)
    v[R_L2B:R_L2B + 16] = l2b.reshape(16, 128)
    return v


_NC_CACHE = {}


def kernel(x_prompt, x_sample, cache_conv, state_hgrn, w_in, b_in, w_dw, b_dw, conv_norm_g, conv_norm_b,
           hg_lower_bounds, hg_norm_g, w_out, ln1_g, ln1_b, w_gate, w_up, w_down, ln2_g, ln2_b):
    f = lambda a: np.ascontiguousarray(np.asarray(a, dtype=np.float32))
    x_prompt, x_sample, cache_conv, state_hgrn = f(x_prompt), f(x_sample), f(cache_conv), f(state_hgrn)
    vec = _vecs(f(b_in)[0], f(w_dw)[0], f(b_dw)[0], f(conv_norm_g)[0], f(conv_norm_b)[0], f(hg_lower_bounds),
                f(hg_norm_g)[0], f(ln1_g)[0], f(ln1_b)[0], f(ln2_g)[0], f(ln2_b)[0])
    con = _consts()
    def colunits(w2d):
        C = w2d.shape[1]
        return np.ascontiguousarray(w2d.reshape(16, 128, C // 128, 128).transpose(2, 1, 0, 3)).reshape(C // 128, 128, 2048)
    wd = f(w_down)[0].reshape(11, 4, 128, 4, 512)
    wdu = np.ascontiguousarray(wd.transpose(0, 3, 2, 1, 4)).reshape(44, 128, 2048)
    wall = np.concatenate([colunits(f(w_in)[0]), colunits(f(w_out)[0]), colunits(f(w_gate)[0]), colunits(f(w_up)[0]), wdu], axis=0)
    assert wall.shape == (NUNIT, 128, 2048)
    shared = {"vecs": vec, "consts": con, "wall": wall}
    in_maps = []
    for core in range(8):
        b, half = core // 2, core % 2
        xm = x_prompt[b, half * 1024:(half + 1) * 1024]
        if half == 0:
            xh = np.zeros((32, D), np.float32)
            xp = np.zeros((1024, D), np.float32)
        else:
            xh = x_prompt[b, 1024 - 32:1024]
            xp = x_prompt[b, 0:1024]
        m = dict(shared)
        m.update({"xm": f(xm), "xh": f(xh), "xp": f(xp), "xs": f(x_sample[core * 4:(core + 1) * 4].reshape(64, D)),
                  "flag": np.full((128, 1), float(half), np.float32), "cc": f(cache_conv[0, core * 4:(core + 1) * 4]),
                  "sh": f(state_hgrn[0, core * 4:(core + 1) * 4])})
        in_maps.append(m)
    if "nc" not in _NC_CACHE:
        _, order = build_nc(None)
        _NC_CACHE["nc"], _ = build_nc(order)
    res = run_bass_kernel_spmd(_NC_CACHE["nc"], in_maps, core_ids=list(range(8)))
    R = res.results
    yp = np.zeros((4, 2048, D), np.float32)
    ysm = np.zeros((32, 16, D), np.float32)
    cp = np.zeros((1, 4, 30, 1024), np.float32)
    hp = np.zeros((1, 4, NH, 128, 128), np.float32)
    csm = np.zeros((1, 32, 30, 1024), np.float32)
    hsm = np.zeros((1, 32, NH, 128, 128), np.float32)
    for core in range(8):
        b, half = core // 2, core % 2
        r = R[core]
        yp[b, half * 1024:(half + 1) * 1024] = r["ym"]
        ysm[core * 4:(core + 1) * 4] = r["ys"].reshape(4, 16, D)
        csm[0, core * 4:(core + 1) * 4] = r["cs"]
        hsm[0, core * 4:(core + 1) * 4] = r["hs"]
        if half == 1:
            cp[0, b] = r["cm"]
            hp[0, b] = r["hm"]
    return (yp, ysm, cp, hp, csm, hsm)
```
